# Optimizing a Trainium2 kernel written in Bass

```python
import math
import jax, jax.numpy as jnp
from jax import lax
import numpy as np

D_MODEL = 1024
BATCH = 8
SEQ = 2048
DEPTH = 4

MIX_WIDTH = D_MODEL
POOL_WIDTH = MIX_WIDTH // 2
POOL_WINDOWS = (2, 4, 8, 16)
POOL_GROUPS = len(POOL_WINDOWS)
POOL_GC = POOL_WIDTH // POOL_GROUPS
N_HEADS = 4
QK_NOPE = 128
QK_ROPE = 64
V_HEAD = 128
QK_HEAD = QK_NOPE + QK_ROPE
MLA_WIDTH = N_HEADS * V_HEAD
Q_LORA = 384
KV_LORA = 256
ROPE_THETA = 10000.0
SOFTMAX_SCALE = 1.0 / math.sqrt(QK_HEAD)
Q_BLOCK = 128
IN_COLS = POOL_WIDTH + Q_LORA + KV_LORA + QK_ROPE
D_FF = 2816
N_SUBLAYERS = 3
EPS = 1e-6

kernel_name = "hybrid_macaron_pool_mla_adaln"


def rms_norm(x, g):
    xf = x.astype(jnp.float32)
    y = xf * lax.rsqrt(jnp.mean(xf * xf, axis=-1, keepdims=True) + EPS)
    return (y * g.astype(jnp.float32)).astype(x.dtype)


def modulate(h, shift, scale):
    return h * (1 + scale[:, None, :]) + shift[:, None, :]


def swiglu(h, w_gate, w_up, w_down):
    return (jax.nn.silu(h @ w_gate) * (h @ w_up)) @ w_down


def rotate_half(x):
    x1, x2 = jnp.split(x, 2, axis=-1)
    return jnp.concatenate([-x2, x1], axis=-1)


def apply_rope(x, cos, sin):
    return x * cos + rotate_half(x) * sin


def causal_multiscale_pool(u, pool_w, pool_scale):
    B, S, C = u.shape
    cs = jnp.cumsum(u.astype(jnp.float32), axis=1)
    pos = jnp.arange(S)
    means = []
    for g, w in enumerate(POOL_WINDOWS):
        csg = cs[..., g * POOL_GC:(g + 1) * POOL_GC]
        lag = jnp.pad(csg, ((0, 0), (w, 0), (0, 0)))[:, :S]
        cnt = jnp.minimum(pos + 1, w).astype(jnp.float32)[None, :, None]
        means.append((csg - lag) / cnt)
    pooled = jnp.stack(means, axis=2).astype(u.dtype)
    diff = pooled - u.reshape(B, S, POOL_GROUPS, POOL_GC)
    y = jnp.einsum('bsgc,gcd->bsgd', diff, pool_w).reshape(B, S, C)
    return y * pool_scale


def mla_attention(cq, ckv, kr, q_a_norm, w_q_b, kv_a_norm, w_kv_b, cos, sin):
    B, S, _ = cq.shape
    q = (rms_norm(cq, q_a_norm) @ w_q_b).reshape(B, S, N_HEADS, QK_HEAD)
    q_nope, q_rope = q[..., :QK_NOPE], q[..., QK_NOPE:]
    q_rope = apply_rope(q_rope, cos[:, :, None, :], sin[:, :, None, :])
    kv = (rms_norm(ckv, kv_a_norm) @ w_kv_b).reshape(B, S, N_HEADS, QK_NOPE + V_HEAD)
    k_nope, v = kv[..., :QK_NOPE], kv[..., QK_NOPE:]
    k_rope = apply_rope(kr, cos, sin)

    nb = S // Q_BLOCK
    qn = q_nope.reshape(B, nb, Q_BLOCK, N_HEADS, QK_NOPE).transpose(1, 0, 3, 2, 4)
    qr = q_rope.reshape(B, nb, Q_BLOCK, N_HEADS, QK_ROPE).transpose(1, 0, 3, 2, 4)
    kn = k_nope.transpose(0, 2, 1, 3)
    vv = v.transpose(0, 2, 1, 3)
    kpos = jnp.arange(S)

    def block(args):
        qn_b, qr_b, i = args
        s = (jnp.einsum('bhqd,bhkd->bhqk', qn_b, kn)
             + jnp.einsum('bhqd,bkd->bhqk', qr_b, k_rope)).astype(jnp.float32) * SOFTMAX_SCALE
        qpos = i * Q_BLOCK + jnp.arange(Q_BLOCK)
        s = jnp.where(qpos[:, None] >= kpos[None, :], s, -jnp.inf)
        p = jax.nn.softmax(s, axis=-1).astype(vv.dtype)
        return jnp.einsum('bhqk,bhkd->bhqd', p, vv)

    o = lax.map(block, (qn, qr, jnp.arange(nb)))
    return o.transpose(1, 0, 3, 2, 4).reshape(B, S, MLA_WIDTH)


def setup_inputs(seed: int = 0) -> dict:
    key = jax.random.key(seed)
    ks = jax.random.split(key, 24)
    f32 = jnp.float32
    nrm = lambda k, shape, s: (jax.random.normal(k, shape, f32) * s)
    gain = lambda k, shape: 1.0 + 0.05 * jax.random.normal(k, shape, f32)
    D, F, L = D_MODEL, D_FF, DEPTH
    x = jax.random.normal(ks[0], (BATCH, SEQ, D), f32)
    c = jax.random.normal(ks[1], (BATCH, D), f32)
    positions = jnp.broadcast_to(jnp.arange(SEQ, dtype=jnp.int32)[None, :], (BATCH, SEQ))
    return {
        "x": x,
        "c": c,
        "positions": positions,
        "ada_w": nrm(ks[2], (L, D, 3 * N_SUBLAYERS * D), 0.5 * D ** -0.5),
        "ada_b": nrm(ks[3], (L, 3 * N_SUBLAYERS * D), 0.01),
        "ffn1_norm": gain(ks[4], (L, D)),
        "ffn1_w_gate": nrm(ks[5], (L, D, F), D ** -0.5),
        "ffn1_w_up": nrm(ks[6], (L, D, F), D ** -0.5),
        "ffn1_w_down": nrm(ks[7], (L, F, D), F ** -0.5),
        "mix_norm": gain(ks[8], (L, D)),
        "w_in": nrm(ks[9], (L, D, IN_COLS), D ** -0.5),
        "pool_w": nrm(ks[10], (L, POOL_GROUPS, POOL_GC, POOL_GC), POOL_GC ** -0.5),
        "pool_scale": gain(ks[11], (L, POOL_WIDTH)),
        "q_a_norm": gain(ks[12], (L, Q_LORA)),
        "w_q_b": nrm(ks[13], (L, Q_LORA, N_HEADS * QK_HEAD), Q_LORA ** -0.5),
        "kv_a_norm": gain(ks[14], (L, KV_LORA)),
        "w_kv_b": nrm(ks[15], (L, KV_LORA, N_HEADS * (QK_NOPE + V_HEAD)), KV_LORA ** -0.5),
        "w_out": nrm(ks[16], (L, MIX_WIDTH, D), MIX_WIDTH ** -0.5),
        "ffn2_norm": gain(ks[17], (L, D)),
        "ffn2_w_gate": nrm(ks[18], (L, D, F), D ** -0.5),
        "ffn2_w_up": nrm(ks[19], (L, D, F), D ** -0.5),
        "ffn2_w_down": nrm(ks[20], (L, F, D), F ** -0.5),
        "final_norm": gain(ks[21], (D,)),
    }


def reference(x, c, positions, ada_w, ada_b, ffn1_norm, ffn1_w_gate, ffn1_w_up, ffn1_w_down,
              mix_norm, w_in, pool_w, pool_scale, q_a_norm, w_q_b, kv_a_norm, w_kv_b, w_out,
              ffn2_norm, ffn2_w_gate, ffn2_w_up, ffn2_w_down, final_norm):
    inv_freq = 1.0 / (ROPE_THETA ** (jnp.arange(0, QK_ROPE, 2, dtype=jnp.float32) / QK_ROPE))
    ang = positions.astype(jnp.float32)[..., None] * inv_freq
    ang = jnp.concatenate([ang, ang], axis=-1)
    cos = jnp.cos(ang).astype(x.dtype)
    sin = jnp.sin(ang).astype(x.dtype)
    c_act = jax.nn.silu(c)

    for l in range(DEPTH):
        mod = c_act @ ada_w[l] + ada_b[l]
        (sh1, sc1, g1, sh2, sc2, g2, sh3, sc3, g3) = jnp.split(mod, 3 * N_SUBLAYERS, axis=-1)

        h = modulate(rms_norm(x, ffn1_norm[l]), sh1, sc1)
        x = x + 0.5 * g1[:, None, :] * swiglu(h, ffn1_w_gate[l], ffn1_w_up[l], ffn1_w_down[l])

        h = modulate(rms_norm(x, mix_norm[l]), sh2, sc2)
        z = h @ w_in[l]
        o1 = POOL_WIDTH
        o2 = o1 + Q_LORA
        o3 = o2 + KV_LORA
        y_pool = causal_multiscale_pool(z[..., :o1], pool_w[l], pool_scale[l])
        y_mla = mla_attention(z[..., o1:o2], z[..., o2:o3], z[..., o3:], q_a_norm[l], w_q_b[l],
                              kv_a_norm[l], w_kv_b[l], cos, sin)
        y = jnp.concatenate([y_pool, y_mla], axis=-1) @ w_out[l]
        x = x + g2[:, None, :] * y

        h = modulate(rms_norm(x, ffn2_norm[l]), sh3, sc3)
        x = x + 0.5 * g3[:, None, :] * swiglu(h, ffn2_w_gate[l], ffn2_w_up[l], ffn2_w_down[l])

    return rms_norm(x, final_norm)
```

```python
import math
from contextlib import ExitStack
import numpy as np
import concourse.bass as bass
import concourse.mybir as mybir
from concourse.bass_utils import run_bass_kernel_spmd

F32 = mybir.dt.float32
BF16 = mybir.dt.bfloat16
I32 = mybir.dt.int32
AF = mybir.ActivationFunctionType
ALU = mybir.AluOpType

D = 1024
T = 2048
L = 4
FF = 2816
NFC = FF // 128
NG = 2
NGRP = NFC // NG
NH = 4
QL = 384
KVL = 256
INC = 1216
EPS = 1e-6
SM_SCALE = 1.0 / math.sqrt(192.0)
TC = 4
NB = 8
DBG = {"stop": 0}

ENGINES = ("pe", "act", "dve", "pool", "sp")
N_DMA_SEMS = 24


class Op:
    __slots__ = ("eng", "fn", "deps", "inc", "event", "is_dma")

    def __init__(self, eng, fn, is_dma):
        self.eng = eng
        self.fn = fn
        self.deps = set()
        self.inc = False
        self.event = None
        self.is_dma = is_dma


class Sched:
    def __init__(self):
        self.streams = {e: [] for e in ENGINES}
        self.last_writer = {}
        self.readers = {}
        self.dma_readers = {}
        self.dma_rr = {e: 0 for e in ENGINES}
        self.dma_last = {}
        self.dma_count = {}

    def op(self, eng, fn, reads=(), writes=(), dma=False):
        o = Op(eng, fn, dma)
        deps = o.deps
        wset = set(writes)
        for t in reads:
            w = self.last_writer.get(t)
            if w is not None:
                deps.add(w)
        for t in wset:
            w = self.last_writer.get(t)
            if w is not None:
                deps.add(w)
            for r in self.readers.get(t, {}).values():
                deps.add(r)
            for r in self.dma_readers.get(t, ()):
                deps.add(r)
        for t in wset:
            self.last_writer[t] = o
            self.readers[t] = {}
            self.dma_readers[t] = []
        for t in reads:
            if t in wset:
                continue
            if dma:
                self.dma_readers.setdefault(t, []).append(o)
            else:
                self.readers.setdefault(t, {})[eng] = o
        deps.discard(o)
        if eng == "pe":
            for d in [d for d in deps if d.eng == "pe" and not d.is_dma]:
                deps.discard(d)
        if dma:
            k = (eng, self.dma_rr[eng])
            self.dma_rr[eng] = (self.dma_rr[eng] + 1) % N_DMA_SEMS
            prev = self.dma_last.get(k)
            if prev is not None:
                deps.add(prev)
            self.dma_count[k] = self.dma_count.get(k, 0) + 16
            o.event = (("dma", k), self.dma_count[k])
            self.dma_last[k] = o
        self.streams[eng].append(o)
        return o

    def finalize(self):
        for e in ENGINES:
            for o in self.streams[e]:
                for d in o.deps:
                    if not d.is_dma:
                        d.inc = True
        for e in ENGINES:
            c = 0
            for o in self.streams[e]:
                if o.is_dma:
                    continue
                if o.inc:
                    c += 1
                    o.event = (("eng", e), c)

    def emit(self, block, sems, final_waits):
        self.finalize()

        def run_stream(e, eng):
            waited = {}
            for o in self.streams[e]:
                need = {}
                for d in o.deps:
                    sk, v = d.event
                    if waited.get(sk, 0) < v and need.get(sk, 0) < v:
                        need[sk] = v
                for sk, v in need.items():
                    eng.wait_ge(sems[sk], v)
                    waited[sk] = v
                ins = o.fn(eng)
                if o.is_dma:
                    ins.then_inc(sems[o.event[0]], 16)
                elif o.inc:
                    ins.then_inc(sems[o.event[0]], 1)
            if e == "sp":
                need = {}
                for d in final_waits:
                    sk, v = d.event
                    if need.get(sk, 0) < v:
                        need[sk] = v
                for sk, v in need.items():
                    eng.wait_ge(sems[sk], v)

        block.tensor(lambda eng: run_stream("pe", eng))
        block.scalar(lambda eng: run_stream("act", eng))
        block.vector(lambda eng: run_stream("dve", eng))
        block.gpsimd(lambda eng: run_stream("pool", eng))
        block.sync(lambda eng: run_stream("sp", eng))


def _cols(n):
    return n // 128


SM_OFF = {}
_o = 0
for _name, _n in (("ada_b", L * 72), ("n1", L * 8), ("n2", L * 8), ("n3", L * 8), ("pscale", L * 4),
                  ("qn", L * 3), ("kvn", L * 2), ("fn", 8), ("invf", 1), ("phase", 1), ("invcnt", 64)):
    SM_OFF[_name] = _o
    _o += _n
NS = _o


def _pm(v):
    v = np.asarray(v, dtype=np.float32)
    return np.ascontiguousarray(v.reshape(-1, 128).T)


def build(layers, first, last):
    nl = len(layers)
    nc = bass.Bass("TRN2", target_bir_lowering=False)

    def din(name, shape, dt=F32):
        return nc.dram_tensor(name, list(shape), dt, kind="ExternalInput").ap()

    xT_in = din("xT", [D, T])
    smalls_d = din("smalls", [128, NS])
    cvec_d = din("cvec", [128, 8])
    cbf_d = din("cbf", [128, 384])
    pos_t = nc.dram_tensor("pos", [1, T], I32, kind="ExternalInput")
    ada_w = din("ada_w", [nl, D, 9 * D])
    wg_d = [din("ffn1_w_gate", [nl, D, FF]), None, din("ffn2_w_gate", [nl, D, FF])]
    wu_d = [din("ffn1_w_up", [nl, D, FF]), None, din("ffn2_w_up", [nl, D, FF])]
    wd_d = [din("ffn1_w_down", [nl, FF, D]), None, din("ffn2_w_down", [nl, FF, D])]
    w_in_d = din("w_in", [nl, D, INC])
    pool_w_d = din("pool_w", [nl, 4, 128, 128])
    w_q_b_d = din("w_q_b", [nl, QL, NH * 192])
    w_kv_b_d = din("w_kv_b", [nl, KVL, NH * 256])
    w_out_d = din("w_out", [nl, D, D])
    yT_out = nc.dram_tensor("yT", [D, T], F32, kind="ExternalOutput").ap()
    dbg_out = nc.dram_tensor("dbg", [4, 128, T], F32, kind="ExternalOutput").ap() if DBG["stop"] else None

    S = Sched()
    es = ExitStack()
    with es:
        def sb(name, shape, dt):
            return es.enter_context(nc.sbuf_tensor(name, list(shape), dt))

        xT = sb("xT_sb", [128, 8, T], F32)
        hT = sb("hT_sb", [128, 8, T], BF16)
        cs = sb("cs_sb", [128, T], F32)
        sm = sb("sm_sb", [128, NS], F32)
        cbf = sb("cbf_sb", [128, 384], BF16)
        cact = sb("cact_sb", [128, 8], BF16)
        cv32 = sb("cv32_sb", [128, 8], F32)
        mods = [sb(f"mods{i}", [128, 72], F32) for i in range(2)]
        coefA = [sb(f"coefA{i}", [128, 24], F32) for i in range(2)]
        coefG = [sb(f"coefG{i}", [128, 24], F32) for i in range(2)]
        gq = sb("gq_sb", [128, L * 3], F32)
        gkv = sb("gkv_sb", [128, L * 2], F32)
        fnA = sb("fnA_sb", [128, 8], F32)
        hal = sb("hal_sb", [128, 4, 16], F32)
        epsb = sb("epsb_sb", [128, 8], F32)
        NFT, NBT, NPT = 5, 6, 4
        ft = [sb(f"ft{i}", [128, 528], F32) for i in range(NFT)]
        bt = [sb(f"bt{i}", [128, 512], BF16) for i in range(NBT)]
        pt = [sb(f"pt{i}", [128, 512], BF16) for i in range(NPT)]
        slot = [sb(f"slot{i}", [128, 6144], BF16) for i in range(2)]
        mid = sb("mid_sb", [128, 2, T], BF16)
        W1 = sb("W1_sb", [128, 8 * INC], BF16)
        wq = sb("wq_sb", [128, 3, NH, 192], BF16)
        wqr = sb("wqr_sb", [128, 3, NH, 128], BF16)
        wkv = sb("wkv_sb", [128, 2, NH, 256], BF16)
        wkr = sb("wkr_sb", [128, 8, 128], BF16)
        poolw = sb("poolw_sb", [128, 4, 128], BF16)
        kTb = sb("kT_sb", [128, T], BF16)
        Vb = sb("V_sb", [128, 16, 128], BF16)
        ps = [es.enter_context(nc.psum_tensor(f"ps{i}", [128, 512], F32)) for i in range(8)]
        sems = {}
        for e in ENGINES:
            sems[("eng", e)] = es.enter_context(nc.semaphore(f"s_{e}"))
        for e in ("pool", "sp"):
            for k in range(N_DMA_SEMS):
                sems[("dma", (e, k))] = es.enter_context(nc.semaphore(f"s_dma_{e}{k}"))
        block = es.enter_context(nc.Block())

        ones = cbf[:, 0:128]
        foldm = cbf[:, 128:256]
        tri = cbf[:, 256:384]

        wgu_v = [slot[i][:, 0:4096].rearrange("p (k f) -> p k f", k=8) for i in range(2)]
        wd_v = [slot[i][:, 4096:6144].rearrange("p (j d) -> p j d", j=2) for i in range(2)]
        cqn = slot[0][:, 0:6144].rearrange("p (j t) -> p j t", j=3)
        ckvn = slot[1][:, 0:4096].rearrange("p (j t) -> p j t", j=2)
        kro = slot[1][:, 4096:6144]
        qn = mid[:, 0, :]
        qr = mid[:, 1, :]
        w_in_v = W1[:, 0:8 * INC].rearrange("p (k f) -> p k f", k=8)
        w_out_v = W1[:, 0:8 * D].rearrange("p (k f) -> p k f", k=8)

        def smc(name, a, b=None):
            o = SM_OFF[name]
            if b is None:
                return sm[:, o + a:o + a + 1]
            return sm[:, o + a:o + b]

        def tcs(tc):
            return slice(tc * 512, (tc + 1) * 512)

        cnt = {"ft": 0, "bt": 0, "pt": 0}

        def nxt(kind, n):
            i = cnt[kind] % n
            cnt[kind] += 1
            return i

        bank_rr = {}

        def bank(role, banks):
            i = bank_rr.get(role, 0)
            bank_rr[role] = i + 1
            return banks[i % len(banks)]

        R0 = [("slot", 0), ("slotu", 0), ("slotd", 0)] + [("cq", j, tc) for j in range(3) for tc in range(TC)]
        R1 = [("slot", 1), ("slotu", 1), ("slotd", 1)] + [("ckv", j, tc) for j in range(2) for tc in range(TC)] + [("kro", tc) for tc in range(TC)]
        RM = [("mid", j, tc) for j in range(2) for tc in range(TC)] + [("qn", tc) for tc in range(TC)] + \
             [("qr", tc) for tc in range(TC)]

        def barrier():
            S.op("dve", lambda e: e.memset(cv32[:, 0:1], 0.0), writes=R0 + R1 + RM + ["cv32"])

        S.op("sp", lambda e: e.dma_start(out=sm[:], in_=smalls_d), writes=["sm"], dma=True)
        S.op("sp", lambda e: e.dma_start(out=cv32[:], in_=cvec_d), writes=["cv32"], dma=True)
        S.op("pool", lambda e: e.dma_start(out=cbf[:], in_=cbf_d), writes=["cbf"], dma=True)
        xv = xT_in.rearrange("(k p) t -> p k t", p=128)
        for k in range(8):
            S.op("sp", lambda e, k=k: e.dma_start(out=xT[:, k, :], in_=xv[:, k, :]),
                 writes=[("x", k, tc) for tc in range(TC)], dma=True)
        pos_i = sb("posi_sb", [128, 512], I32)
        for tc in range(TC):
            S.op("sp", lambda e, tc=tc: e.dma_start(
                out=pos_i[:], in_=bass.AP(pos_t, tc * 512, [[0, 128], [1, 512]])),
                writes=["posi"], dma=True)
            S.op("dve", lambda e, tc=tc: e.tensor_copy(cs[:, tcs(tc)], pos_i[:]), reads=["posi"], writes=[("cs", tc)])
            S.op("dve", lambda e, tc=tc: e.tensor_scalar(cs[:, tcs(tc)], cs[:, tcs(tc)], smc("invf", 0), smc("phase", 0),
                                                         ALU.mult, ALU.add), reads=["sm"], writes=[("cs", tc)])
            S.op("dve", lambda e, tc=tc: e.tensor_scalar(pos_i[:], cs[:, tcs(tc)], 1.0 / (2.0 * math.pi), None, ALU.mult),
                 reads=[("cs", tc)], writes=["posi"])
            S.op("dve", lambda e, tc=tc: e.tensor_copy(ft[0][:, 0:512], pos_i[:]), reads=["posi"], writes=[("ft", 0)])
            S.op("dve", lambda e, tc=tc: e.scalar_tensor_tensor(cs[:, tcs(tc)], ft[0][:, 0:512], -2.0 * math.pi, cs[:, tcs(tc)],
                                                                ALU.mult, ALU.add), reads=[("ft", 0)], writes=[("cs", tc)])
            S.op("dve", lambda e, tc=tc: e.tensor_scalar(ft[0][:, 0:512], cs[:, tcs(tc)], math.pi, 2.0 * math.pi, ALU.is_gt, ALU.mult),
                 reads=[("cs", tc)], writes=[("ft", 0)])
            S.op("dve", lambda e, tc=tc: e.tensor_tensor(cs[:, tcs(tc)], cs[:, tcs(tc)], ft[0][:, 0:512], ALU.subtract),
                 reads=[("ft", 0)], writes=[("cs", tc)])
            S.op("dve", lambda e, tc=tc: e.tensor_scalar(ft[0][:, 0:512], cs[:, tcs(tc)], -math.pi, 2.0 * math.pi, ALU.is_lt, ALU.mult),
                 reads=[("cs", tc)], writes=[("ft", 0)])
            S.op("dve", lambda e, tc=tc: e.tensor_tensor(cs[:, tcs(tc)], cs[:, tcs(tc)], ft[0][:, 0:512], ALU.add),
                 reads=[("ft", 0)], writes=[("cs", tc)])
            S.op("dve", lambda e, tc=tc: e.tensor_scalar(cs[:, tcs(tc)], cs[:, tcs(tc)], -3.14159, 3.14159, ALU.max, ALU.min),
                 writes=[("cs", tc)])
            S.op("act", lambda e, tc=tc: e.activation(cs[:, tcs(tc)], cs[:, tcs(tc)], AF.Sin), writes=[("cs", tc)])
        S.op("act", lambda e: e.activation(cact[:], cv32[:], AF.Silu), reads=["cv32"], writes=["cact"])
        o_qn, o_kvn, o_fn = SM_OFF["qn"], SM_OFF["kvn"], SM_OFF["fn"]
        S.op("dve", lambda e: e.tensor_scalar(gq[:], sm[:, o_qn:o_qn + L * 3], math.sqrt(QL), None, ALU.mult),
             reads=["sm"], writes=["gq"])
        S.op("dve", lambda e: e.tensor_scalar(gkv[:], sm[:, o_kvn:o_kvn + L * 2], math.sqrt(KVL), None, ALU.mult),
             reads=["sm"], writes=["gkv"])
        S.op("dve", lambda e: e.tensor_scalar(fnA[:], sm[:, o_fn:o_fn + 8], math.sqrt(D), None, ALU.mult),
             reads=["sm"], writes=["fnA"])
        S.op("dve", lambda e: e.memset(hal[:], 0.0), writes=["hal_init"])
        for dt0 in (256, 384, 1024):
            S.op("dve", lambda e, dt0=dt0: e.memset(epsb[:, dt0 // 128 - 2:dt0 // 128 - 1], float(dt0) * EPS), writes=["epsb"])

        def ada_dma(li, pc):
            src = ada_w[li, :, pc * 1024:(pc + 1) * 1024].rearrange("(k p) f -> p k f", p=128)
            S.op("pool", lambda e: e.dma_start(out=w_out_v, in_=src), writes=["W1"], dma=True)

        def ada_mm(li, pc):
            for cc in range(8):
                col = pc * 8 + cc
                for k in range(8):
                    S.op("pe", lambda e, cc=cc, k=k, col=col: e.matmul(
                        ps[7][:, col:col + 1], w_out_v[:, k, cc * 128:(cc + 1) * 128], cact[:, k:k + 1],
                        start=(k == 0), stop=(k == 7)), reads=["W1", "cact"], writes=[("ps", 7)])

        def ada_finish(li, subs):
            l = layers[li]
            par = l % 2
            for s in subs:
                ob = SM_OFF["ada_b"] + l * 72 + 24 * s
                S.op("dve", lambda e, s=s, ob=ob: e.tensor_tensor(
                    mods[par][:, 24 * s:24 * s + 24], ps[7][:, 24 * s:24 * s + 24], sm[:, ob:ob + 24], ALU.add),
                    reads=[("ps", 7), "sm"], writes=[("mods", par, s)])
                on = SM_OFF[("n1", "n2", "n3")[s]] + l * 8
                S.op("dve", lambda e, s=s, on=on: e.scalar_tensor_tensor(
                    coefA[par][:, s * 8:(s + 1) * 8], mods[par][:, (3 * s + 1) * 8:(3 * s + 2) * 8], 1.0,
                    sm[:, on:on + 8], ALU.add, ALU.mult), reads=[("mods", par, s), "sm"], writes=[("cA", par, s)])
                S.op("dve", lambda e, s=s: e.tensor_scalar(
                    coefA[par][:, s * 8:(s + 1) * 8], coefA[par][:, s * 8:(s + 1) * 8], math.sqrt(D), None, ALU.mult),
                    writes=[("cA", par, s)])
                cg = (0.5, 1.0, 0.5)[s]
                S.op("dve", lambda e, s=s, cg=cg: e.tensor_scalar(
                    coefG[par][:, s * 8:(s + 1) * 8], mods[par][:, (3 * s + 2) * 8:(3 * s + 3) * 8], cg, None, ALU.mult),
                    reads=[("mods", par, s)], writes=[("cG", par, s)])

        def stat_rs(tc, srcs, dtot, read_tokens):
            n = len(srcs)
            for i, (ap, rt) in enumerate(zip(srcs, read_tokens)):
                b = nxt("bt", NBT)
                S.op("act", lambda e, ap=ap, b=b: e.activation(bt[b][:], ap, AF.Square), reads=[rt], writes=[("bt", b)])
                S.op("pe", lambda e, b=b, i=i: e.matmul(ps[7][:], ones, bt[b][:], start=(i == 0), stop=(i == n - 1)),
                     reads=[("bt", b), "cbf"], writes=[("ps", 7)])
            r = nxt("ft", NFT)
            S.op("act", lambda e, r=r: e.activation(ft[r][:, 0:512], ps[7][:], AF.Ln, bias=epsb[:, int(dtot) // 128 - 2:int(dtot) // 128 - 1]),
                 reads=[("ps", 7), "epsb"], writes=[("ft", r)])
            S.op("act", lambda e, r=r: e.activation(ft[r][:, 0:512], ft[r][:, 0:512], AF.Exp, scale=-0.5), writes=[("ft", r)])
            return r

        def norm_tc(l, s, tc):
            par = l % 2
            r = stat_rs(tc, [xT[:, k, tcs(tc)] for k in range(8)], D, [("x", k, tc) for k in range(8)])
            for k in range(8):
                t = nxt("ft", NFT)
                if t == r:
                    t = nxt("ft", NFT)
                S.op("dve", lambda e, k=k, t=t, r=r: e.scalar_tensor_tensor(
                    ft[t][:, 0:512], xT[:, k, tcs(tc)], coefA[par][:, s * 8 + k:s * 8 + k + 1], ft[r][:, 0:512],
                    ALU.mult, ALU.mult), reads=[("x", k, tc), ("cA", par, s), ("ft", r)], writes=[("ft", t)])
                S.op("act", lambda e, k=k, t=t: e.activation(
                    hT[:, k, tcs(tc)], ft[t][:, 0:512], AF.Identity,
                    bias=mods[par][:, 3 * s * 8 + k:3 * s * 8 + k + 1]),
                    reads=[("ft", t), ("mods", par, s)], writes=[("h", k, tc)])

        def norm_mod(l, s):
            for tc in range(TC):
                norm_tc(l, s, tc)

        def final_tc(tc):
            r = stat_rs(tc, [xT[:, k, tcs(tc)] for k in range(8)], D, [("x", k, tc) for k in range(8)])
            for k in range(8):
                S.op("dve", lambda e, k=k, r=r: e.scalar_tensor_tensor(
                    xT[:, k, tcs(tc)], xT[:, k, tcs(tc)], fnA[:, k:k + 1], ft[r][:, 0:512], ALU.mult, ALU.mult),
                    reads=["fnA", ("ft", r)], writes=[("x", k, tc)])

        ffn_seq = []
        for li in range(nl):
            for s in (0, 2):
                for gi in range(NGRP):
                    ffn_seq.append((li, s, gi))
        loaded = [False] * len(ffn_seq)

        def ffn_load(idx):
            if idx >= len(ffn_seq) or loaded[idx]:
                return
            loaded[idx] = True
            li, s, gi = ffn_seq[idx]
            sl = idx % 2
            f0 = gi * NG * 128
            gsrc = wg_d[s][li, :, f0:f0 + 256].rearrange("(k p) f -> p k f", p=128)
            usrc = wu_d[s][li, :, f0:f0 + 256].rearrange("(k p) f -> p k f", p=128)
            dsrc = wd_d[s][li, f0:f0 + 256, :].rearrange("(j p) d -> p j d", p=128)
            S.op("pool", lambda e: e.dma_start(out=wgu_v[sl][:, :, 0:256], in_=gsrc), writes=[("slot", sl)], dma=True)
            S.op("pool", lambda e: e.dma_start(out=wgu_v[sl][:, :, 256:512], in_=usrc), writes=[("slotu", sl)], dma=True)
            S.op("pool", lambda e: e.dma_start(out=wd_v[sl], in_=dsrc), writes=[("slotd", sl)], dma=True)

        def ffn_group(l, s, idx, after=None):
            par = l % 2
            sl = idx % 2
            wtok = [("slot", sl), ("slotu", sl), ("slotd", sl)]

            def gu(tc, js=range(NG)):
                for j in js:
                    bg = bank("gu", [0, 1, 2, 3])
                    bu = bank("gu", [0, 1, 2, 3])
                    for k in range(8):
                        S.op("pe", lambda e, k=k, j=j, bg=bg: e.matmul(
                            ps[bg][:], wgu_v[sl][:, k, j * 128:(j + 1) * 128], hT[:, k, tcs(tc)],
                            start=(k == 0), stop=(k == 7)), reads=wtok + [("h", k, tc)], writes=[("ps", bg)])
                    for k in range(8):
                        S.op("pe", lambda e, k=k, j=j, bu=bu: e.matmul(
                            ps[bu][:], wgu_v[sl][:, k, 256 + j * 128:256 + (j + 1) * 128], hT[:, k, tcs(tc)],
                            start=(k == 0), stop=(k == 7)), reads=wtok + [("h", k, tc)], writes=[("ps", bu)])
                    t = nxt("ft", NFT)
                    S.op("act", lambda e, t=t, bg=bg: e.activation(ft[t][:, 0:512], ps[bg][:], AF.Silu),
                         reads=[("ps", bg)], writes=[("ft", t)])
                    S.op("dve", lambda e, t=t, bu=bu, j=j: e.tensor_tensor(
                        mid[:, j, tcs(tc)], ft[t][:, 0:512], ps[bu][:], ALU.mult),
                        reads=[("ft", t), ("ps", bu)], writes=[("mid", j, tc)])

            def down(tc, dcs=range(8)):
                for dc in dcs:
                    bo = bank("dn", [4, 5, 6])
                    for j in range(NG):
                        S.op("pe", lambda e, j=j, dc=dc, bo=bo: e.matmul(
                            ps[bo][:], wd_v[sl][:, j, dc * 128:(dc + 1) * 128], mid[:, j, tcs(tc)],
                            start=(j == 0), stop=(j == NG - 1)), reads=wtok + [("mid", j, tc)], writes=[("ps", bo)])
                    S.op("dve", lambda e, dc=dc, bo=bo: e.scalar_tensor_tensor(
                        xT[:, dc, tcs(tc)], ps[bo][:], coefG[par][:, s * 8 + dc:s * 8 + dc + 1], xT[:, dc, tcs(tc)],
                        ALU.mult, ALU.add), reads=[("ps", bo), ("cG", par, s)], writes=[("x", dc, tc)])

            for tc in range(TC):
                for j in range(NG):
                    gu(tc, [j])
                    if tc >= 1:
                        down(tc - 1, range(4 * j, 4 * j + 4))
                if tc >= 2 and after is not None:
                    after(tc - 2)
            down(TC - 1)
            if after is not None:
                after(TC - 2)
                after(TC - 1)

        def mixer_load(li):
            S.op("pool", lambda e: e.dma_start(out=w_in_v, in_=w_in_d[li].rearrange("(k p) f -> p k f", p=128)),
                 writes=["W1"], dma=True)
            S.op("pool", lambda e: e.dma_start(out=wq[:], in_=w_q_b_d[li].rearrange("(k p) (h c) -> p k h c", p=128, h=NH)),
                 writes=["wq"], dma=True)
            S.op("pool", lambda e: e.dma_start(out=wkv[:], in_=w_kv_b_d[li].rearrange("(k p) (h c) -> p k h c", p=128, h=NH)),
                 writes=["wkv"], dma=True)
            S.op("pool", lambda e: e.dma_start(out=poolw[:], in_=pool_w_d[li].rearrange("g c d -> c g d")),
                 writes=["poolw"], dma=True)
            S.op("dve", lambda e: e.tensor_copy(wqr[:, :, :, 0:64], wq[:, :, :, 128:192]), reads=["wq"], writes=["wqr"])
            S.op("dve", lambda e: e.tensor_scalar(wqr[:, :, :, 64:96], wq[:, :, :, 160:192], -1.0, None, ALU.mult),
                 reads=["wq"], writes=["wqr"])
            S.op("dve", lambda e: e.tensor_copy(wqr[:, :, :, 96:128], wq[:, :, :, 128:160]), reads=["wq"], writes=["wqr"])
            S.op("dve", lambda e: e.tensor_copy(wkr[:, :, 0:64], w_in_v[:, :, 1152:1216]), reads=["W1"], writes=["wkr"])
            S.op("dve", lambda e: e.tensor_scalar(wkr[:, :, 64:96], w_in_v[:, :, 1184:1216], -1.0, None, ALU.mult),
                 reads=["W1"], writes=["wkr"])
            S.op("dve", lambda e: e.tensor_copy(wkr[:, :, 96:128], w_in_v[:, :, 1152:1184]), reads=["W1"], writes=["wkr"])

        def phase_a(l, tc, prev_pend):
            hr = [("h", k, tc) for k in range(8)]
            def proj(b, c0, lhs=None):
                for k in range(8):
                    lt = w_in_v[:, k, c0:c0 + 128] if lhs is None else lhs[:, k, :]
                    S.op("pe", lambda e, k=k, lt=lt, b=b: e.matmul(ps[b][:], lt, hT[:, k, tcs(tc)],
                                                                  start=(k == 0), stop=(k == 7)),
                         reads=["W1", "wkr", ("h", k, tc)], writes=[("ps", b)])
            for j in range(2):
                proj(j, 896 + j * 128)
            for j in range(3):
                proj(2 + j, 512 + j * 128)
            pool_y(l, *prev_pend) if prev_pend else None
            r = stat_rs(tc, [ps[j][:] for j in range(2)], KVL, [("ps", j) for j in range(2)])
            for j in range(2):
                S.op("dve", lambda e, j=j, r=r: e.scalar_tensor_tensor(
                    ckvn[:, j, tcs(tc)], ps[j][:], gkv[:, l * 2 + j:l * 2 + j + 1], ft[r][:, 0:512], ALU.mult, ALU.mult),
                    reads=[("ps", j), "gkv", ("ft", r)], writes=[("ckv", j, tc)])
            b = bank("misc", [5, 6])
            proj(b, 0, lhs=wkr)
            kp = nxt("bt", NBT)
            S.op("dve", lambda e, b=b, kp=kp: e.tensor_tensor(bt[kp][:], ps[b][:], cs[:, tcs(tc)], ALU.mult),
                 reads=[("ps", b), ("cs", tc)], writes=[("bt", kp)])
            r = stat_rs(tc, [ps[2 + j][:] for j in range(3)], QL, [("ps", 2 + j) for j in range(3)])
            for j in range(3):
                S.op("dve", lambda e, j=j, r=r: e.scalar_tensor_tensor(
                    cqn[:, j, tcs(tc)], ps[2 + j][:], gq[:, l * 3 + j:l * 3 + j + 1], ft[r][:, 0:512], ALU.mult, ALU.mult),
                    reads=[("ps", 2 + j), "gq", ("ft", r)], writes=[("cq", j, tc)])
            b2 = bank("misc", [5, 6])
            S.op("pe", lambda e, b2=b2, kp=kp: e.matmul(ps[b2][:], foldm, bt[kp][:], start=True, stop=True),
                 reads=[("bt", kp), "cbf"], writes=[("ps", b2)])
            S.op("act", lambda e, b2=b2: e.activation(kro[:, tcs(tc)], ps[b2][:], AF.Copy),
                 reads=[("ps", b2)], writes=[("kro", tc)])
            pend = []
            for g in range(4):
                w = 2 << g
                b = bank("misc", [5, 6])
                proj(b, g * 128)
                U, T1, T2 = nxt("ft", NFT), nxt("ft", NFT), nxt("ft", NFT)
                S.op("dve", lambda e, U=U, g=g: e.tensor_copy(ft[U][:, 0:16], hal[:, g, :]),
                     reads=[("hal", g), "hal_init"], writes=[("ft", U)])
                S.op("act", lambda e, U=U, b=b: e.activation(ft[U][:, 16:528], ps[b][:], AF.Copy),
                     reads=[("ps", b)], writes=[("ft", U)])
                S.op("dve", lambda e, U=U, g=g: e.tensor_copy(hal[:, g, :], ft[U][:, 512:528]),
                     reads=[("ft", U)], writes=[("hal", g)])
                bufs = [U, T1, T2, T1, T2]
                lo = [0, 1, 3, 7, 15]
                for st in range(1, g + 2):
                    src, dst = bufs[st - 1], bufs[st]
                    sh = 1 << (st - 1)
                    a0 = lo[st]
                    S.op("dve", lambda e, src=src, dst=dst, sh=sh, a0=a0: e.tensor_tensor(
                        ft[dst][:, a0:528], ft[src][:, a0:528], ft[src][:, a0 - sh:528 - sh], ALU.add),
                        reads=[("ft", src)], writes=[("ft", dst)])
                Sb = bufs[g + 1]
                dt_ = nxt("bt", NBT)
                S.op("dve", lambda e, Sb=Sb, w=w: e.tensor_scalar(
                    ft[Sb][:, 16:528], ft[Sb][:, 16:528], 1.0 / w, None, ALU.mult), writes=[("ft", Sb)])
                if tc == 0:
                    oc = SM_OFF["invcnt"] + g * 16
                    S.op("dve", lambda e, Sb=Sb, oc=oc: e.tensor_tensor(
                        ft[Sb][:, 16:32], ft[Sb][:, 16:32], sm[:, oc:oc + 16], ALU.mult),
                        reads=["sm"], writes=[("ft", Sb)])
                S.op("dve", lambda e, Sb=Sb, U=U, dt_=dt_: e.tensor_tensor(
                    bt[dt_][:], ft[Sb][:, 16:528], ft[U][:, 16:528], ALU.subtract),
                    reads=[("ft", Sb), ("ft", U)], writes=[("bt", dt_)])
                pend.append((g, dt_))
            return (tc, pend)

        def pool_y(l, tc, pend):
            for g, dt_ in pend:
                by = bank("misc", [5, 6])
                S.op("pe", lambda e, by=by, dt_=dt_, g=g: e.matmul(ps[by][:], poolw[:, g, :], bt[dt_][:], start=True, stop=True),
                     reads=[("bt", dt_), "poolw"], writes=[("ps", by)])
                osc = SM_OFF["pscale"] + l * 4 + g
                S.op("act", lambda e, by=by, g=g, osc=osc: e.activation(
                    hT[:, g, tcs(tc)], ps[by][:], AF.Copy, scale=sm[:, osc:osc + 1]),
                    reads=[("ps", by), "sm"], writes=[("h", g, tc)])

        def head_proj(h):
            pb = [0, 1, 2, 3, 4, 5, 6]
            for tc in range(TC):
                b = bank("hp", pb)
                for j in range(2):
                    S.op("pe", lambda e, j=j, b=b, tc=tc: e.matmul(ps[b][:], wkv[:, j, h, 0:128], ckvn[:, j, tcs(tc)],
                                                                 start=(j == 0), stop=(j == 1)),
                         reads=["wkv", ("ckv", j, tc)], writes=[("ps", b)])
                S.op("act", lambda e, b=b, tc=tc: e.activation(kTb[:, tcs(tc)], ps[b][:], AF.Copy),
                     reads=[("ps", b)], writes=[("kT", tc)])
            for t4 in range(4):
                b = bank("hp", pb)
                for i in range(4):
                    tk = t4 * 4 + i
                    for j in range(2):
                        S.op("pe", lambda e, j=j, b=b, i=i, tk=tk: e.matmul(
                            ps[b][:, i * 128:(i + 1) * 128], ckvn[:, j, tk * 128:(tk + 1) * 128], wkv[:, j, h, 128:256],
                            start=(j == 0), stop=(j == 1)), reads=["wkv", ("ckv", j, tk // 4)], writes=[("ps", b)])
                S.op("act", lambda e, b=b, t4=t4: e.activation(
                    Vb[:, t4 * 4:(t4 + 1) * 4, :], ps[b][:].rearrange("p (i d) -> p i d", i=4), AF.Copy),
                    reads=[("ps", b)], writes=[("V", t4)])
            for tc in range(TC):
                b = bank("hp", pb)
                for j in range(3):
                    S.op("pe", lambda e, j=j, b=b, tc=tc: e.matmul(ps[b][:], wq[:, j, h, 0:128], cqn[:, j, tcs(tc)],
                                                                 start=(j == 0), stop=(j == 2)),
                         reads=["wq", ("cq", j, tc)], writes=[("ps", b)])
                S.op("act", lambda e, b=b, tc=tc: e.activation(qn[:, tcs(tc)], ps[b][:], AF.Copy),
                     reads=[("ps", b)], writes=[("qn", tc)])
                b = bank("hp", pb)
                for j in range(3):
                    S.op("pe", lambda e, j=j, b=b, tc=tc: e.matmul(ps[b][:], wqr[:, j, h, :], cqn[:, j, tcs(tc)],
                                                                 start=(j == 0), stop=(j == 2)),
                         reads=["wqr", ("cq", j, tc)], writes=[("ps", b)])
                S.op("dve", lambda e, b=b, tc=tc: e.tensor_tensor(qr[:, tcs(tc)], ps[b][:], cs[:, tcs(tc)], ALU.mult),
                     reads=[("ps", b), ("cs", tc)], writes=[("qr", tc)])

        def attention(h):
            for Q in range(TC):
                att_q(h, Q)

        def att_q(h, Q):
            if True:
                q0 = Q * 512
                nk = 4 * Q + 4
                bo = bank("ao", [3, 4])
                bd = bank("ad", [5, 6])
                sinfo = {}

                def score(kc):
                    j = kc - 4 * Q
                    c0 = 128 * j if j > 0 else 0
                    bs = bank("as", [0, 1, 2])
                    k0 = kc * 128
                    S.op("pe", lambda e: e.matmul(ps[bs][:, c0:512], kTb[:, k0:k0 + 128], qn[:, q0 + c0:q0 + 512],
                                                  start=True, stop=False),
                         reads=[("kT", kc // 4), ("qn", Q)], writes=[("ps", bs)])
                    S.op("pe", lambda e: e.matmul(ps[bs][:, c0:512], kro[:, k0:k0 + 128], qr[:, q0 + c0:q0 + 512],
                                                  start=False, stop=True),
                         reads=[("kro", kc // 4), ("qr", Q)], writes=[("ps", bs)])
                    p = nxt("pt", NPT)
                    S.op("act", lambda e: e.activation(pt[p][:, c0:512], ps[bs][:, c0:512], AF.Exp, scale=SM_SCALE),
                         reads=[("ps", bs)], writes=[("pt", p)])
                    if j >= 0:
                        S.op("dve", lambda e: e.tensor_tensor(pt[p][:, c0:c0 + 128], pt[p][:, c0:c0 + 128], tri, ALU.mult),
                             reads=["cbf"], writes=[("pt", p)])
                    sinfo[kc] = (p, c0)

                def pv(kc):
                    p, c0 = sinfo[kc]
                    S.op("pe", lambda e: e.matmul(ps[bo][:, c0:512], Vb[:, kc, :], pt[p][:, c0:512],
                                                  start=(kc == 0), stop=(kc == nk - 1)),
                         reads=[("V", kc // 4), ("pt", p)], writes=[("ps", bo)])
                    S.op("pe", lambda e: e.matmul(ps[bd][:, c0:512], ones, pt[p][:, c0:512],
                                                  start=(kc == 0), stop=(kc == nk - 1)),
                         reads=["cbf", ("pt", p)], writes=[("ps", bd)])

                score(0)
                if nk > 1:
                    score(1)
                for kc in range(nk):
                    if kc + 2 < nk:
                        score(kc + 2)
                    pv(kc)
                r = nxt("ft", NFT)
                S.op("act", lambda e, r=r, bd=bd: e.activation(ft[r][:, 0:512], ps[bd][:], AF.Ln),
                     reads=[("ps", bd)], writes=[("ft", r)])
                S.op("act", lambda e, r=r: e.activation(ft[r][:, 0:512], ft[r][:, 0:512], AF.Exp, scale=-1.0), writes=[("ft", r)])
                S.op("dve", lambda e, r=r, bo=bo, Q=Q: e.tensor_tensor(hT[:, 4 + h, tcs(Q)], ps[bo][:], ft[r][:, 0:512], ALU.mult),
                     reads=[("ps", bo), ("ft", r)], writes=[("h", 4 + h, Q)])

        def out_proj(l, li, after=None):
            par = l % 2
            for tc in range(TC):
                for dc in range(8):
                    b = bank("op", [0, 1, 2, 3, 4, 5, 6])
                    for k in range(8):
                        S.op("pe", lambda e, k=k, dc=dc, b=b, tc=tc: e.matmul(
                            ps[b][:], w_out_v[:, k, dc * 128:(dc + 1) * 128], hT[:, k, tcs(tc)],
                            start=(k == 0), stop=(k == 7)), reads=["W1", ("h", k, tc)], writes=[("ps", b)])
                    S.op("dve", lambda e, dc=dc, b=b, tc=tc: e.scalar_tensor_tensor(
                        xT[:, dc, tcs(tc)], ps[b][:], coefG[par][:, 8 + dc:8 + dc + 1], xT[:, dc, tcs(tc)],
                        ALU.mult, ALU.add), reads=[("ps", b), ("cG", par, 1)], writes=[("x", dc, tc)])
                if after is not None and tc >= 1:
                    after(tc - 1)
            if after is not None:
                after(TC - 1)

        ffn_load(0)
        ffn_load(1)
        for pc in range(3):
            ada_dma(0, pc)
            ada_mm(0, pc)
        ada_finish(0, [0])
        gidx = 0
        stop = DBG["stop"]
        dbg_ops = []

        def dump(i, ap, reads):
            dbg_ops.append(S.op("pool", lambda e: e.dma_start(out=dbg_out[i, :, 0:ap.shape[1]], in_=ap), reads=reads, dma=True))

        if stop == 1:
            dump(0, cs[:], [("cs", tc) for tc in range(TC)])
            dump(1, mods[0][:], [("mods", 0, 0)])
            dump(2, coefA[0][:], [("cA", 0, s_) for s_ in range(3)])
            dump(3, coefG[0][:], [("cG", 0, s_) for s_ in range(3)])
        for li, l in enumerate(layers):
            if stop == 1:
                break
            if stop == 7:
                ffn_load(0)
                norm_mod(l, 0)
                for k in range(8):
                    S.op("pe", lambda e, k=k: e.matmul(ps[0][:], slot[0][:, k * 512:k * 512 + 128], hT[:, k, 0:512], start=(k == 0), stop=(k == 7)),
                         reads=[("slot", 0), ("slotu", 0), ("slotd", 0), ("h", k, 0)], writes=[("ps", 0)])
                S.op("dve", lambda e: e.tensor_copy(ft[0][:, 0:512], ps[0][:]), reads=[("ps", 0)], writes=[("ft", 0)])
                for i_, k_ in enumerate((2, 4, 6, 7)):
                    S.op("pe", lambda e, i_=i_, k_=k_: e.matmul(ps[1 + i_][:], slot[0][:, k_ * 512:k_ * 512 + 128], hT[:, k_, 0:512], start=True, stop=True),
                         reads=[("h", k_, 0), ("slot", 0)], writes=[("ps", 1 + i_)])
                    S.op("dve", lambda e, i_=i_: e.tensor_copy(ft[i_][:, 0:512], ps[1 + i_][:]), reads=[("ps", 1 + i_)], writes=[("ft", i_)])
                for i_ in range(4):
                    dump(i_, ft[i_][:, 0:512], [("ft", i_)])
                break
            if stop == 2:
                norm_mod(l, 0)
                for k_ in range(4):
                    dump(k_, hT[:, k_, :], [("h", k_, tc) for tc in range(TC)])
                break
            ffn_load(gidx)
            ffn_load(gidx + 1)
            if li > 0:
                mixer_load(li)
            else:
                norm_mod(l, 0)
            for gi in range(NGRP):
                ffn_group(l, 0, gidx, after=(lambda tc, l=l: norm_tc(l, 1, tc)) if gi == NGRP - 1 else None)
                if li == 0:
                    if 1 <= gi <= 6:
                        ada_mm(0, 2 + gi)
                    if gi <= 5:
                        ada_dma(0, 3 + gi)
                    if gi == 6:
                        ada_finish(0, [1, 2])
                        mixer_load(li)
                if stop == 6:
                    dump(0, mid[:, 0, :], [("mid", 0, tc) for tc in range(TC)])
                    dump(1, mid[:, 1, :], [("mid", 1, tc) for tc in range(TC)])
                    dump(2, xT[:, 0, :], [("x", 0, tc) for tc in range(TC)])
                    dump(3, wgu_v[0][:, 1, :], [("slot", 0), ("slotu", 0)])
                    break
                if gi + 2 < NGRP:
                    ffn_load(gidx + 2)
                gidx += 1
            if stop == 6:
                break
            barrier()
            if stop == 3:
                break
            S.op("dve", lambda e: e.memset(hal[:], 0.0), writes=["hal_init"] + [("hal", g) for g in range(4)])
            pp = None
            for tc in range(TC):
                pp = phase_a(l, tc, pp)
            pool_y(l, *pp)
            S.op("pool", lambda e, li=li: e.dma_start(out=w_out_v, in_=w_out_d[li].rearrange("(k p) f -> p k f", p=128)),
                 writes=["W1"], dma=True)
            for h in range(NH):
                head_proj(h)
                attention(h)
            if stop == 4:
                for k_ in range(4):
                    dump(k_, hT[:, k_, :], [("h", k_, tc) for tc in range(TC)])
                break
            barrier()
            ffn_load(gidx)
            ffn_load(gidx + 1)
            out_proj(l, li, after=lambda tc, l=l: norm_tc(l, 2, tc))
            if stop == 5:
                break
            for gi in range(NGRP):
                aft = None
                if gi == NGRP - 1:
                    if li + 1 < nl:
                        aft = lambda tc, l=l: norm_tc(l + 1, 0, tc)
                    elif last:
                        aft = final_tc
                ffn_group(l, 2, gidx, after=aft)
                ffn_load(gidx + 2)
                gidx += 1
                if li + 1 < nl:
                    if 1 <= gi <= 9:
                        ada_mm(li + 1, gi - 1)
                    if gi <= 8:
                        ada_dma(li + 1, gi)
                    if gi == 9:
                        ada_finish(li + 1, [0, 1, 2])
        finals = []
        yv = yT_out.rearrange("(k p) t -> p k t", p=128)
        for k in range(8):
            finals.append(S.op("sp", lambda e, k=k: e.dma_start(out=yv[:, k, :], in_=xT[:, k, :]),
                               reads=[("x", k, tc) for tc in range(TC)], dma=True))
        S.emit(block, sems, finals + dbg_ops)
    return nc


def _host_consts():
    r = np.arange(128)
    inv_freq = 1.0 / (10000.0 ** (np.arange(0, 64, 2, dtype=np.float32) / 64.0))
    invf = inv_freq[(r % 64) % 32].astype(np.float32)
    phase = np.where(r < 64, math.pi / 2, 0.0).astype(np.float32)
    invcnt = np.zeros((128, 64), np.float32)
    for g in range(4):
        w = 2 << g
        for t in range(16):
            invcnt[:, g * 16 + t] = float(w) / min(t + 1, w)
    ones = np.ones((128, 128), np.float32)
    fold = np.zeros((128, 128), np.float32)
    for p in range(128):
        for m in range(128):
            if p % 64 == m % 64:
                fold[p, m] = 1.0
    tri = (r[None, :] >= r[:, None]).astype(np.float32)
    cbf = np.concatenate([ones, fold, tri], axis=1)
    return invf, phase, invcnt, np.ascontiguousarray(cbf)


LAUNCH_GROUPS = [[0, 1, 2, 3]]


def kernel(x, c, positions, ada_w, ada_b, ffn1_norm, ffn1_w_gate, ffn1_w_up, ffn1_w_down,
           mix_norm, w_in, pool_w, pool_scale, q_a_norm, w_q_b, kv_a_norm, w_kv_b, w_out,
           ffn2_norm, ffn2_w_gate, ffn2_w_up, ffn2_w_down, final_norm):
    f = lambda a: np.ascontiguousarray(np.asarray(a, dtype=np.float32))
    invf, phase, invcnt, cbf = _host_consts()
    smalls = np.zeros((128, NS), np.float32)

    def put(name, arr):
        o = SM_OFF[name]
        smalls[:, o:o + arr.shape[1]] = arr

    put("ada_b", np.concatenate([_pm(np.asarray(ada_b)[l]) for l in range(L)], axis=1))
    put("n1", np.concatenate([_pm(np.asarray(ffn1_norm)[l]) for l in range(L)], axis=1))
    put("n2", np.concatenate([_pm(np.asarray(mix_norm)[l]) for l in range(L)], axis=1))
    put("n3", np.concatenate([_pm(np.asarray(ffn2_norm)[l]) for l in range(L)], axis=1))
    put("pscale", np.concatenate([_pm(np.asarray(pool_scale)[l]) for l in range(L)], axis=1))
    put("qn", np.concatenate([_pm(np.asarray(q_a_norm)[l]) for l in range(L)], axis=1))
    put("kvn", np.concatenate([_pm(np.asarray(kv_a_norm)[l]) for l in range(L)], axis=1))
    put("fn", _pm(np.asarray(final_norm)))
    put("invf", invf[:, None])
    put("phase", phase[:, None])
    put("invcnt", invcnt)

    x = np.asarray(x, dtype=np.float32)
    c = np.asarray(c, dtype=np.float32)
    positions = np.asarray(positions, dtype=np.int32)
    cur = [np.ascontiguousarray(x[b].T) for b in range(NB)]
    wfull = dict(ada_w=f(ada_w), ffn1_w_gate=f(ffn1_w_gate), ffn1_w_up=f(ffn1_w_up), ffn1_w_down=f(ffn1_w_down),
                 w_in=f(w_in), pool_w=f(pool_w), w_q_b=f(w_q_b), w_kv_b=f(w_kv_b), w_out=f(w_out),
                 ffn2_w_gate=f(ffn2_w_gate), ffn2_w_up=f(ffn2_w_up), ffn2_w_down=f(ffn2_w_down))
    ng = len(LAUNCH_GROUPS)
    for gi, layers in enumerate(LAUNCH_GROUPS):
        nc = build(layers, gi == 0, gi == ng - 1)
        wl = {k: np.ascontiguousarray(v[layers[0]:layers[-1] + 1]) for k, v in wfull.items()}
        in_maps = []
        for b in range(NB):
            m = dict(wl)
            m.update(xT=cur[b], smalls=smalls, cvec=_pm(c[b]), cbf=cbf,
                     pos=np.ascontiguousarray(positions[b][None, :]))
            in_maps.append(m)
        res = run_bass_kernel_spmd(nc, in_maps, core_ids=list(range(NB)))
        cur = [np.asarray(res.results[b]["yT"]) for b in range(NB)]
    out = np.stack([cur[b].T for b in range(NB)], axis=0).astype(np.float32)
    return out
```

```python
import math
from contextlib import ExitStack
import numpy as np
import concourse.bass as bass
import concourse.mybir as mybir
from concourse.bass_utils import run_bass_kernel_spmd

F32 = mybir.dt.float32
BF16 = mybir.dt.bfloat16
I32 = mybir.dt.int32
AF = mybir.ActivationFunctionType
ALU = mybir.AluOpType

D = 1024
T = 2048
L = 4
FF = 2816
NFC = FF // 128
NG = 2
NGRP = NFC // NG
NH = 4
QL = 384
KVL = 256
INC = 1216
EPS = 1e-6
SM_SCALE = 1.0 / math.sqrt(192.0)
TC = 4
NB = 8
DBG = {"stop": 0}

ENGINES = ("pe", "act", "dve", "pool", "sp")
N_DMA_SEMS = 24


class Op:
    __slots__ = ("eng", "fn", "deps", "inc", "event", "is_dma")

    def __init__(self, eng, fn, is_dma):
        self.eng = eng
        self.fn = fn
        self.deps = set()
        self.inc = False
        self.event = None
        self.is_dma = is_dma


class Sched:
    def __init__(self):
        self.streams = {e: [] for e in ENGINES}
        self.last_writer = {}
        self.readers = {}
        self.dma_readers = {}
        self.dma_rr = {e: 0 for e in ENGINES}
        self.dma_last = {}
        self.dma_count = {}

    def op(self, eng, fn, reads=(), writes=(), dma=False):
        o = Op(eng, fn, dma)
        deps = o.deps
        wset = set(writes)
        for t in reads:
            w = self.last_writer.get(t)
            if w is not None:
                deps.add(w)
        for t in wset:
            w = self.last_writer.get(t)
            if w is not None:
                deps.add(w)
            for r in self.readers.get(t, {}).values():
                deps.add(r)
            for r in self.dma_readers.get(t, ()):
                deps.add(r)
        for t in wset:
            self.last_writer[t] = o
            self.readers[t] = {}
            self.dma_readers[t] = []
        for t in reads:
            if t in wset:
                continue
            if dma:
                self.dma_readers.setdefault(t, []).append(o)
            else:
                self.readers.setdefault(t, {})[eng] = o
        deps.discard(o)
        if eng == "pe":
            for d in [d for d in deps if d.eng == "pe" and not d.is_dma]:
                deps.discard(d)
        if dma:
            k = (eng, self.dma_rr[eng])
            self.dma_rr[eng] = (self.dma_rr[eng] + 1) % N_DMA_SEMS
            prev = self.dma_last.get(k)
            if prev is not None:
                deps.add(prev)
            self.dma_count[k] = self.dma_count.get(k, 0) + 16
            o.event = (("dma", k), self.dma_count[k])
            self.dma_last[k] = o
        self.streams[eng].append(o)
        return o

    def finalize(self):
        for e in ENGINES:
            for o in self.streams[e]:
                for d in o.deps:
                    if not d.is_dma:
                        d.inc = True
        for e in ENGINES:
            c = 0
            for o in self.streams[e]:
                if o.is_dma:
                    continue
                if o.inc:
                    c += 1
                    o.event = (("eng", e), c)

    def emit(self, block, sems, final_waits):
        self.finalize()

        def run_stream(e, eng):
            waited = {}
            for o in self.streams[e]:
                need = {}
                for d in o.deps:
                    sk, v = d.event
                    if waited.get(sk, 0) < v and need.get(sk, 0) < v:
                        need[sk] = v
                for sk, v in need.items():
                    eng.wait_ge(sems[sk], v)
                    waited[sk] = v
                ins = o.fn(eng)
                if o.is_dma:
                    ins.then_inc(sems[o.event[0]], 16)
                elif o.inc:
                    ins.then_inc(sems[o.event[0]], 1)
            if e == "sp":
                need = {}
                for d in final_waits:
                    sk, v = d.event
                    if need.get(sk, 0) < v:
                        need[sk] = v
                for sk, v in need.items():
                    eng.wait_ge(sems[sk], v)

        block.tensor(lambda eng: run_stream("pe", eng))
        block.scalar(lambda eng: run_stream("act", eng))
        block.vector(lambda eng: run_stream("dve", eng))
        block.gpsimd(lambda eng: run_stream("pool", eng))
        block.sync(lambda eng: run_stream("sp", eng))


def _cols(n):
    return n // 128


SM_OFF = {}
_o = 0
for _name, _n in (("ada_b", L * 72), ("n1", L * 8), ("n2", L * 8), ("n3", L * 8), ("pscale", L * 4),
                  ("qn", L * 3), ("kvn", L * 2), ("fn", 8), ("invf", 1), ("phase", 1), ("invcnt", 64)):
    SM_OFF[_name] = _o
    _o += _n
NS = _o


def _pm(v):
    v = np.asarray(v, dtype=np.float32)
    return np.ascontiguousarray(v.reshape(-1, 128).T)


def build(layers, first, last):
    nl = len(layers)
    nc = bass.Bass("TRN2", target_bir_lowering=False)

    def din(name, shape, dt=F32):
        return nc.dram_tensor(name, list(shape), dt, kind="ExternalInput").ap()

    xT_in = din("xT", [D, T])
    smalls_d = din("smalls", [128, NS])
    cvec_d = din("cvec", [128, 8])
    cbf_d = din("cbf", [128, 384])
    pos_t = nc.dram_tensor("pos", [1, T], I32, kind="ExternalInput")
    ada_w = din("ada_w", [nl, D, 9 * D])
    wg_d = [din("ffn1_w_gate", [nl, D, FF]), None, din("ffn2_w_gate", [nl, D, FF])]
    wu_d = [din("ffn1_w_up", [nl, D, FF]), None, din("ffn2_w_up", [nl, D, FF])]
    wd_d = [din("ffn1_w_down", [nl, FF, D]), None, din("ffn2_w_down", [nl, FF, D])]
    w_in_d = din("w_in", [nl, D, INC])
    pool_w_d = din("pool_w", [nl, 4, 128, 128])
    w_q_b_d = din("w_q_b", [nl, QL, NH * 192])
    w_kv_b_d = din("w_kv_b", [nl, KVL, NH * 256])
    w_out_d = din("w_out", [nl, D, D])
    yT_out = nc.dram_tensor("yT", [D, T], F32, kind="ExternalOutput").ap()
    dbg_out = nc.dram_tensor("dbg", [4, 128, T], F32, kind="ExternalOutput").ap() if DBG["stop"] else None

    S = Sched()
    es = ExitStack()
    with es:
        def sb(name, shape, dt):
            return es.enter_context(nc.sbuf_tensor(name, list(shape), dt))

        xT = sb("xT_sb", [128, 8, T], F32)
        hT = sb("hT_sb", [128, 8, T], BF16)
        cs = sb("cs_sb", [128, T], F32)
        sm = sb("sm_sb", [128, NS], F32)
        cbf = sb("cbf_sb", [128, 384], BF16)
        cact = sb("cact_sb", [128, 8], BF16)
        cv32 = sb("cv32_sb", [128, 8], F32)
        mods = [sb(f"mods{i}", [128, 72], F32) for i in range(2)]
        coefA = [sb(f"coefA{i}", [128, 24], F32) for i in range(2)]
        coefG = [sb(f"coefG{i}", [128, 24], F32) for i in range(2)]
        gq = sb("gq_sb", [128, L * 3], F32)
        gkv = sb("gkv_sb", [128, L * 2], F32)
        fnA = sb("fnA_sb", [128, 8], F32)
        hal = sb("hal_sb", [128, 4, 16], F32)
        epsb = sb("epsb_sb", [128, 8], F32)
        NFT, NBT, NPT = 5, 6, 4
        ft = [sb(f"ft{i}", [128, 528], F32) for i in range(NFT)]
        bt = [sb(f"bt{i}", [128, 512], BF16) for i in range(NBT)]
        pt = [sb(f"pt{i}", [128, 512], BF16) for i in range(NPT)]
        slot = [sb(f"slot{i}", [128, 6144], BF16) for i in range(2)]
        mid = sb("mid_sb", [128, 2, T], BF16)
        W1 = sb("W1_sb", [128, 8 * INC], BF16)
        wq = sb("wq_sb", [128, 3, NH, 192], BF16)
        wqr = sb("wqr_sb", [128, 3, NH, 128], BF16)
        wkv = sb("wkv_sb", [128, 2, NH, 256], BF16)
        wkr = sb("wkr_sb", [128, 8, 128], BF16)
        poolw = sb("poolw_sb", [128, 4, 128], BF16)
        kTb = sb("kT_sb", [128, T], BF16)
        Vb = sb("V_sb", [128, 16, 128], BF16)
        ps = [es.enter_context(nc.psum_tensor(f"ps{i}", [128, 512], F32)) for i in range(8)]
        sems = {}
        for e in ENGINES:
            sems[("eng", e)] = es.enter_context(nc.semaphore(f"s_{e}"))
        for e in ("pool", "sp"):
            for k in range(N_DMA_SEMS):
                sems[("dma", (e, k))] = es.enter_context(nc.semaphore(f"s_dma_{e}{k}"))
        block = es.enter_context(nc.Block())

        ones = cbf[:, 0:128]
        foldm = cbf[:, 128:256]
        tri = cbf[:, 256:384]

        wgu_v = [slot[i][:, 0:4096].rearrange("p (k f) -> p k f", k=8) for i in range(2)]
        wd_v = [slot[i][:, 4096:6144].rearrange("p (j d) -> p j d", j=2) for i in range(2)]
        cqn = slot[0][:, 0:6144].rearrange("p (j t) -> p j t", j=3)
        ckvn = slot[1][:, 0:4096].rearrange("p (j t) -> p j t", j=2)
        kro = slot[1][:, 4096:6144]
        qn = mid[:, 0, :]
        qr = mid[:, 1, :]
        w_in_v = W1[:, 0:8 * INC].rearrange("p (k f) -> p k f", k=8)
        w_out_v = W1[:, 0:8 * D].rearrange("p (k f) -> p k f", k=8)

        def smc(name, a, b=None):
            o = SM_OFF[name]
            if b is None:
                return sm[:, o + a:o + a + 1]
            return sm[:, o + a:o + b]

        def tcs(tc):
            return slice(tc * 512, (tc + 1) * 512)

        cnt = {"ft": 0, "bt": 0, "pt": 0}

        def nxt(kind, n):
            i = cnt[kind] % n
            cnt[kind] += 1
            return i

        bank_rr = {}

        def bank(role, banks):
            i = bank_rr.get(role, 0)
            bank_rr[role] = i + 1
            return banks[i % len(banks)]

        R0 = [("slot", 0), ("slotu", 0), ("slotd", 0)] + [("cq", j, tc) for j in range(3) for tc in range(TC)]
        R1 = [("slot", 1), ("slotu", 1), ("slotd", 1)] + [("ckv", j, tc) for j in range(2) for tc in range(TC)] + [("kro", tc) for tc in range(TC)]
        RM = [("mid", j, tc) for j in range(2) for tc in range(TC)] + [("qn", tc) for tc in range(TC)] + \
             [("qr", tc) for tc in range(TC)]

        def barrier():
            S.op("dve", lambda e: e.memset(cv32[:, 0:1], 0.0), writes=R0 + R1 + RM + ["cv32"])

        S.op("sp", lambda e: e.dma_start(out=sm[:], in_=smalls_d), writes=["sm"], dma=True)
        S.op("sp", lambda e: e.dma_start(out=cv32[:], in_=cvec_d), writes=["cv32"], dma=True)
        S.op("pool", lambda e: e.dma_start(out=cbf[:], in_=cbf_d), writes=["cbf"], dma=True)
        xv = xT_in.rearrange("(k p) t -> p k t", p=128)
        for k in range(8):
            S.op("sp", lambda e, k=k: e.dma_start(out=xT[:, k, :], in_=xv[:, k, :]),
                 writes=[("x", k, tc) for tc in range(TC)], dma=True)
        pos_i = sb("posi_sb", [128, 512], I32)
        for tc in range(TC):
            S.op("sp", lambda e, tc=tc: e.dma_start(
                out=pos_i[:], in_=bass.AP(pos_t, tc * 512, [[0, 128], [1, 512]])),
                writes=["posi"], dma=True)
            S.op("dve", lambda e, tc=tc: e.tensor_copy(cs[:, tcs(tc)], pos_i[:]), reads=["posi"], writes=[("cs", tc)])
            S.op("dve", lambda e, tc=tc: e.tensor_scalar(cs[:, tcs(tc)], cs[:, tcs(tc)], smc("invf", 0), smc("phase", 0),
                                                         ALU.mult, ALU.add), reads=["sm"], writes=[("cs", tc)])
            S.op("dve", lambda e, tc=tc: e.tensor_scalar(pos_i[:], cs[:, tcs(tc)], 1.0 / (2.0 * math.pi), None, ALU.mult),
                 reads=[("cs", tc)], writes=["posi"])
            S.op("dve", lambda e, tc=tc: e.tensor_copy(ft[0][:, 0:512], pos_i[:]), reads=["posi"], writes=[("ft", 0)])
            S.op("dve", lambda e, tc=tc: e.scalar_tensor_tensor(cs[:, tcs(tc)], ft[0][:, 0:512], -2.0 * math.pi, cs[:, tcs(tc)],
                                                                ALU.mult, ALU.add), reads=[("ft", 0)], writes=[("cs", tc)])
            S.op("dve", lambda e, tc=tc: e.tensor_scalar(ft[0][:, 0:512], cs[:, tcs(tc)], math.pi, 2.0 * math.pi, ALU.is_gt, ALU.mult),
                 reads=[("cs", tc)], writes=[("ft", 0)])
            S.op("dve", lambda e, tc=tc: e.tensor_tensor(cs[:, tcs(tc)], cs[:, tcs(tc)], ft[0][:, 0:512], ALU.subtract),
                 reads=[("ft", 0)], writes=[("cs", tc)])
            S.op("dve", lambda e, tc=tc: e.tensor_scalar(ft[0][:, 0:512], cs[:, tcs(tc)], -math.pi, 2.0 * math.pi, ALU.is_lt, ALU.mult),
                 reads=[("cs", tc)], writes=[("ft", 0)])
            S.op("dve", lambda e, tc=tc: e.tensor_tensor(cs[:, tcs(tc)], cs[:, tcs(tc)], ft[0][:, 0:512], ALU.add),
                 reads=[("ft", 0)], writes=[("cs", tc)])
            S.op("dve", lambda e, tc=tc: e.tensor_scalar(cs[:, tcs(tc)], cs[:, tcs(tc)], -3.14159, 3.14159, ALU.max, ALU.min),
                 writes=[("cs", tc)])
            S.op("act", lambda e, tc=tc: e.activation(cs[:, tcs(tc)], cs[:, tcs(tc)], AF.Sin), writes=[("cs", tc)])
        S.op("act", lambda e: e.activation(cact[:], cv32[:], AF.Silu), reads=["cv32"], writes=["cact"])
        o_qn, o_kvn, o_fn = SM_OFF["qn"], SM_OFF["kvn"], SM_OFF["fn"]
        S.op("dve", lambda e: e.tensor_scalar(gq[:], sm[:, o_qn:o_qn + L * 3], math.sqrt(QL), None, ALU.mult),
             reads=["sm"], writes=["gq"])
        S.op("dve", lambda e: e.tensor_scalar(gkv[:], sm[:, o_kvn:o_kvn + L * 2], math.sqrt(KVL), None, ALU.mult),
             reads=["sm"], writes=["gkv"])
        S.op("dve", lambda e: e.tensor_scalar(fnA[:], sm[:, o_fn:o_fn + 8], math.sqrt(D), None, ALU.mult),
             reads=["sm"], writes=["fnA"])
        S.op("dve", lambda e: e.memset(hal[:], 0.0), writes=["hal_init"])
        for dt0 in (256, 384, 1024):
            S.op("dve", lambda e, dt0=dt0: e.memset(epsb[:, dt0 // 128 - 2:dt0 // 128 - 1], float(dt0) * EPS), writes=["epsb"])

        def ada_dma(li, pc):
            src = ada_w[li, :, pc * 1024:(pc + 1) * 1024].rearrange("(k p) f -> p k f", p=128)
            S.op("pool", lambda e: e.dma_start(out=w_out_v, in_=src), writes=["W1"], dma=True)

        def ada_mm(li, pc):
            for cc in range(8):
                col = pc * 8 + cc
                for k in range(8):
                    S.op("pe", lambda e, cc=cc, k=k, col=col: e.matmul(
                        ps[7][:, col:col + 1], w_out_v[:, k, cc * 128:(cc + 1) * 128], cact[:, k:k + 1],
                        start=(k == 0), stop=(k == 7)), reads=["W1", "cact"], writes=[("ps", 7)])

        def ada_finish(li, subs):
            l = layers[li]
            par = l % 2
            for s in subs:
                ob = SM_OFF["ada_b"] + l * 72 + 24 * s
                S.op("dve", lambda e, s=s, ob=ob: e.tensor_tensor(
                    mods[par][:, 24 * s:24 * s + 24], ps[7][:, 24 * s:24 * s + 24], sm[:, ob:ob + 24], ALU.add),
                    reads=[("ps", 7), "sm"], writes=[("mods", par, s)])
                on = SM_OFF[("n1", "n2", "n3")[s]] + l * 8
                S.op("dve", lambda e, s=s, on=on: e.scalar_tensor_tensor(
                    coefA[par][:, s * 8:(s + 1) * 8], mods[par][:, (3 * s + 1) * 8:(3 * s + 2) * 8], 1.0,
                    sm[:, on:on + 8], ALU.add, ALU.mult), reads=[("mods", par, s), "sm"], writes=[("cA", par, s)])
                S.op("dve", lambda e, s=s: e.tensor_scalar(
                    coefA[par][:, s * 8:(s + 1) * 8], coefA[par][:, s * 8:(s + 1) * 8], math.sqrt(D), None, ALU.mult),
                    writes=[("cA", par, s)])
                cg = (0.5, 1.0, 0.5)[s]
                S.op("dve", lambda e, s=s, cg=cg: e.tensor_scalar(
                    coefG[par][:, s * 8:(s + 1) * 8], mods[par][:, (3 * s + 2) * 8:(3 * s + 3) * 8], cg, None, ALU.mult),
                    reads=[("mods", par, s)], writes=[("cG", par, s)])

        def stat_rs(tc, srcs, dtot, read_tokens):
            n = len(srcs)
            for i, (ap, rt) in enumerate(zip(srcs, read_tokens)):
                b = nxt("bt", NBT)
                S.op("act", lambda e, ap=ap, b=b: e.activation(bt[b][:], ap, AF.Square), reads=[rt], writes=[("bt", b)])
                S.op("pe", lambda e, b=b, i=i: e.matmul(ps[7][:], ones, bt[b][:], start=(i == 0), stop=(i == n - 1)),
                     reads=[("bt", b), "cbf"], writes=[("ps", 7)])
            r = nxt("ft", NFT)
            S.op("act", lambda e, r=r: e.activation(ft[r][:, 0:512], ps[7][:], AF.Ln, bias=epsb[:, int(dtot) // 128 - 2:int(dtot) // 128 - 1]),
                 reads=[("ps", 7), "epsb"], writes=[("ft", r)])
            S.op("act", lambda e, r=r: e.activation(ft[r][:, 0:512], ft[r][:, 0:512], AF.Exp, scale=-0.5), writes=[("ft", r)])
            return r

        def norm_tc(l, s, tc):
            par = l % 2
            r = stat_rs(tc, [xT[:, k, tcs(tc)] for k in range(8)], D, [("x", k, tc) for k in range(8)])
            for k in range(8):
                t = nxt("ft", NFT)
                if t == r:
                    t = nxt("ft", NFT)
                S.op("dve", lambda e, k=k, t=t, r=r: e.scalar_tensor_tensor(
                    ft[t][:, 0:512], xT[:, k, tcs(tc)], coefA[par][:, s * 8 + k:s * 8 + k + 1], ft[r][:, 0:512],
                    ALU.mult, ALU.mult), reads=[("x", k, tc), ("cA", par, s), ("ft", r)], writes=[("ft", t)])
                S.op("act", lambda e, k=k, t=t: e.activation(
                    hT[:, k, tcs(tc)], ft[t][:, 0:512], AF.Identity,
                    bias=mods[par][:, 3 * s * 8 + k:3 * s * 8 + k + 1]),
                    reads=[("ft", t), ("mods", par, s)], writes=[("h", k, tc)])

        def norm_mod(l, s):
            for tc in range(TC):
                norm_tc(l, s, tc)

        def final_tc(tc):
            r = stat_rs(tc, [xT[:, k, tcs(tc)] for k in range(8)], D, [("x", k, tc) for k in range(8)])
            for k in range(8):
                S.op("dve", lambda e, k=k, r=r: e.scalar_tensor_tensor(
                    xT[:, k, tcs(tc)], xT[:, k, tcs(tc)], fnA[:, k:k + 1], ft[r][:, 0:512], ALU.mult, ALU.mult),
                    reads=["fnA", ("ft", r)], writes=[("x", k, tc)])

        ffn_seq = []
        for li in range(nl):
            for s in (0, 2):
                for gi in range(NGRP):
                    ffn_seq.append((li, s, gi))
        loaded = [False] * len(ffn_seq)

        def ffn_load(idx):
            if idx >= len(ffn_seq) or loaded[idx]:
                return
            loaded[idx] = True
            li, s, gi = ffn_seq[idx]
            sl = idx % 2
            f0 = gi * NG * 128
            gsrc = wg_d[s][li, :, f0:f0 + 256].rearrange("(k p) f -> p k f", p=128)
            usrc = wu_d[s][li, :, f0:f0 + 256].rearrange("(k p) f -> p k f", p=128)
            dsrc = wd_d[s][li, f0:f0 + 256, :].rearrange("(j p) d -> p j d", p=128)
            S.op("pool", lambda e: e.dma_start(out=wgu_v[sl][:, :, 0:256], in_=gsrc), writes=[("slot", sl)], dma=True)
            S.op("pool", lambda e: e.dma_start(out=wgu_v[sl][:, :, 256:512], in_=usrc), writes=[("slotu", sl)], dma=True)
            S.op("pool", lambda e: e.dma_start(out=wd_v[sl], in_=dsrc), writes=[("slotd", sl)], dma=True)

        def ffn_group(l, s, idx, after=None):
            par = l % 2
            sl = idx % 2
            wtok = [("slot", sl), ("slotu", sl), ("slotd", sl)]

            def gu(tc, js=range(NG)):
                for j in js:
                    bg = bank("gu", [0, 1, 2, 3])
                    bu = bank("gu", [0, 1, 2, 3])
                    for k in range(8):
                        S.op("pe", lambda e, k=k, j=j, bg=bg: e.matmul(
                            ps[bg][:], wgu_v[sl][:, k, j * 128:(j + 1) * 128], hT[:, k, tcs(tc)],
                            start=(k == 0), stop=(k == 7)), reads=wtok + [("h", k, tc)], writes=[("ps", bg)])
                    for k in range(8):
                        S.op("pe", lambda e, k=k, j=j, bu=bu: e.matmul(
                            ps[bu][:], wgu_v[sl][:, k, 256 + j * 128:256 + (j + 1) * 128], hT[:, k, tcs(tc)],
                            start=(k == 0), stop=(k == 7)), reads=wtok + [("h", k, tc)], writes=[("ps", bu)])
                    t = nxt("ft", NFT)
                    S.op("act", lambda e, t=t, bg=bg: e.activation(ft[t][:, 0:512], ps[bg][:], AF.Silu),
                         reads=[("ps", bg)], writes=[("ft", t)])
                    S.op("dve", lambda e, t=t, bu=bu, j=j: e.tensor_tensor(
                        mid[:, j, tcs(tc)], ft[t][:, 0:512], ps[bu][:], ALU.mult),
                        reads=[("ft", t), ("ps", bu)], writes=[("mid", j, tc)])

            def down(tc, dcs=range(8)):
                for dc in dcs:
                    bo = bank("dn", [4, 5, 6])
                    for j in range(NG):
                        S.op("pe", lambda e, j=j, dc=dc, bo=bo: e.matmul(
                            ps[bo][:], wd_v[sl][:, j, dc * 128:(dc + 1) * 128], mid[:, j, tcs(tc)],
                            start=(j == 0), stop=(j == NG - 1)), reads=wtok + [("mid", j, tc)], writes=[("ps", bo)])
                    S.op("dve", lambda e, dc=dc, bo=bo: e.scalar_tensor_tensor(
                        xT[:, dc, tcs(tc)], ps[bo][:], coefG[par][:, s * 8 + dc:s * 8 + dc + 1], xT[:, dc, tcs(tc)],
                        ALU.mult, ALU.add), reads=[("ps", bo), ("cG", par, s)], writes=[("x", dc, tc)])

            for tc in range(TC):
                for j in range(NG):
                    gu(tc, [j])
                    if tc >= 1:
                        down(tc - 1, range(4 * j, 4 * j + 4))
                if tc >= 2 and after is not None:
                    after(tc - 2)
            down(TC - 1)
            if after is not None:
                after(TC - 2)
                after(TC - 1)

        def mixer_load(li):
            S.op("pool", lambda e: e.dma_start(out=w_in_v, in_=w_in_d[li].rearrange("(k p) f -> p k f", p=128)),
                 writes=["W1"], dma=True)
            S.op("pool", lambda e: e.dma_start(out=wq[:], in_=w_q_b_d[li].rearrange("(k p) (h c) -> p k h c", p=128, h=NH)),
                 writes=["wq"], dma=True)
            S.op("pool", lambda e: e.dma_start(out=wkv[:], in_=w_kv_b_d[li].rearrange("(k p) (h c) -> p k h c", p=128, h=NH)),
                 writes=["wkv"], dma=True)
            S.op("pool", lambda e: e.dma_start(out=poolw[:], in_=pool_w_d[li].rearrange("g c d -> c g d")),
                 writes=["poolw"], dma=True)

        def mixer_build():
            S.op("dve", lambda e: e.tensor_copy(wqr[:, :, :, 0:64], wq[:, :, :, 128:192]), reads=["wq"], writes=["wqr"])
            S.op("dve", lambda e: e.tensor_scalar(wqr[:, :, :, 64:96], wq[:, :, :, 160:192], -1.0, None, ALU.mult),
                 reads=["wq"], writes=["wqr"])
            S.op("dve", lambda e: e.tensor_copy(wqr[:, :, :, 96:128], wq[:, :, :, 128:160]), reads=["wq"], writes=["wqr"])
            S.op("dve", lambda e: e.tensor_copy(wkr[:, :, 0:64], w_in_v[:, :, 1152:1216]), reads=["W1"], writes=["wkr"])
            S.op("dve", lambda e: e.tensor_scalar(wkr[:, :, 64:96], w_in_v[:, :, 1184:1216], -1.0, None, ALU.mult),
                 reads=["W1"], writes=["wkr"])
            S.op("dve", lambda e: e.tensor_copy(wkr[:, :, 96:128], w_in_v[:, :, 1152:1184]), reads=["W1"], writes=["wkr"])

        def phase_a(l, tc, prev_pend):
            hr = [("h", k, tc) for k in range(8)]
            def proj(b, c0, lhs=None):
                for k in range(8):
                    lt = w_in_v[:, k, c0:c0 + 128] if lhs is None else lhs[:, k, :]
                    S.op("pe", lambda e, k=k, lt=lt, b=b: e.matmul(ps[b][:], lt, hT[:, k, tcs(tc)],
                                                                  start=(k == 0), stop=(k == 7)),
                         reads=["W1", "wkr", ("h", k, tc)], writes=[("ps", b)])
            for j in range(2):
                proj(j, 896 + j * 128)
            for j in range(3):
                proj(2 + j, 512 + j * 128)
            r = stat_rs(tc, [ps[j][:] for j in range(2)], KVL, [("ps", j) for j in range(2)])
            for j in range(2):
                S.op("dve", lambda e, j=j, r=r: e.scalar_tensor_tensor(
                    ckvn[:, j, tcs(tc)], ps[j][:], gkv[:, l * 2 + j:l * 2 + j + 1], ft[r][:, 0:512], ALU.mult, ALU.mult),
                    reads=[("ps", j), "gkv", ("ft", r)], writes=[("ckv", j, tc)])
            b = bank("misc", [5, 6])
            proj(b, 0, lhs=wkr)
            kp = nxt("bt", NBT)
            S.op("dve", lambda e, b=b, kp=kp: e.tensor_tensor(bt[kp][:], ps[b][:], cs[:, tcs(tc)], ALU.mult),
                 reads=[("ps", b), ("cs", tc)], writes=[("bt", kp)])
            r = stat_rs(tc, [ps[2 + j][:] for j in range(3)], QL, [("ps", 2 + j) for j in range(3)])
            for j in range(3):
                S.op("dve", lambda e, j=j, r=r: e.scalar_tensor_tensor(
                    cqn[:, j, tcs(tc)], ps[2 + j][:], gq[:, l * 3 + j:l * 3 + j + 1], ft[r][:, 0:512], ALU.mult, ALU.mult),
                    reads=[("ps", 2 + j), "gq", ("ft", r)], writes=[("cq", j, tc)])
            b2 = bank("misc", [5, 6])
            S.op("pe", lambda e, b2=b2, kp=kp: e.matmul(ps[b2][:], foldm, bt[kp][:], start=True, stop=True),
                 reads=[("bt", kp), "cbf"], writes=[("ps", b2)])
            S.op("act", lambda e, b2=b2: e.activation(kro[:, tcs(tc)], ps[b2][:], AF.Copy),
                 reads=[("ps", b2)], writes=[("kro", tc)])
            if prev_pend:
                pool_y(l, *prev_pend)
            pend = []
            for g in range(4):
                w = 2 << g
                b = bank("misc", [5, 6])
                proj(b, g * 128)
                U, T1, T2 = nxt("ft", NFT), nxt("ft", NFT), nxt("ft", NFT)
                S.op("dve", lambda e, U=U, g=g: e.tensor_copy(ft[U][:, 0:16], hal[:, g, :]),
                     reads=[("hal", g), "hal_init"], writes=[("ft", U)])
                S.op("act", lambda e, U=U, b=b: e.activation(ft[U][:, 16:528], ps[b][:], AF.Copy),
                     reads=[("ps", b)], writes=[("ft", U)])
                S.op("dve", lambda e, U=U, g=g: e.tensor_copy(hal[:, g, :], ft[U][:, 512:528]),
                     reads=[("ft", U)], writes=[("hal", g)])
                bufs = [U, T1, T2, T1, T2]
                lo = [0, 1, 3, 7, 15]
                for st in range(1, g + 2):
                    src, dst = bufs[st - 1], bufs[st]
                    sh = 1 << (st - 1)
                    a0 = lo[st]
                    S.op("dve", lambda e, src=src, dst=dst, sh=sh, a0=a0: e.tensor_tensor(
                        ft[dst][:, a0:528], ft[src][:, a0:528], ft[src][:, a0 - sh:528 - sh], ALU.add),
                        reads=[("ft", src)], writes=[("ft", dst)])
                Sb = bufs[g + 1]
                dt_ = g
                S.op("dve", lambda e, Sb=Sb, w=w: e.tensor_scalar(
                    ft[Sb][:, 16:528], ft[Sb][:, 16:528], 1.0 / w, None, ALU.mult), writes=[("ft", Sb)])
                if tc == 0:
                    oc = SM_OFF["invcnt"] + g * 16
                    S.op("dve", lambda e, Sb=Sb, oc=oc: e.tensor_tensor(
                        ft[Sb][:, 16:32], ft[Sb][:, 16:32], sm[:, oc:oc + 16], ALU.mult),
                        reads=["sm"], writes=[("ft", Sb)])
                S.op("dve", lambda e, Sb=Sb, U=U, dt_=dt_: e.tensor_tensor(
                    pt[dt_][:], ft[Sb][:, 16:528], ft[U][:, 16:528], ALU.subtract),
                    reads=[("ft", Sb), ("ft", U)], writes=[("pt", dt_)])
                pend.append((g, dt_))
            return (tc, pend)

        def pool_y(l, tc, pend):
            for g, dt_ in pend:
                by = bank("misc", [5, 6])
                S.op("pe", lambda e, by=by, dt_=dt_, g=g: e.matmul(ps[by][:], poolw[:, g, :], pt[dt_][:], start=True, stop=True),
                     reads=[("pt", dt_), "poolw"], writes=[("ps", by)])
                osc = SM_OFF["pscale"] + l * 4 + g
                S.op("act", lambda e, by=by, g=g, osc=osc: e.activation(
                    hT[:, g, tcs(tc)], ps[by][:], AF.Copy, scale=sm[:, osc:osc + 1]),
                    reads=[("ps", by), "sm"], writes=[("h", g, tc)])

        def head_proj(h):
            pb = [0, 1, 2, 3, 4, 5, 6]
            for tc in range(TC):
                b = bank("hp", pb)
                for j in range(2):
                    S.op("pe", lambda e, j=j, b=b, tc=tc: e.matmul(ps[b][:], wkv[:, j, h, 0:128], ckvn[:, j, tcs(tc)],
                                                                 start=(j == 0), stop=(j == 1)),
                         reads=["wkv", ("ckv", j, tc)], writes=[("ps", b)])
                S.op("act", lambda e, b=b, tc=tc: e.activation(kTb[:, tcs(tc)], ps[b][:], AF.Copy),
                     reads=[("ps", b)], writes=[("kT", tc)])
            for t4 in range(4):
                b = bank("hp", pb)
                for i in range(4):
                    tk = t4 * 4 + i
                    for j in range(2):
                        S.op("pe", lambda e, j=j, b=b, i=i, tk=tk: e.matmul(
                            ps[b][:, i * 128:(i + 1) * 128], ckvn[:, j, tk * 128:(tk + 1) * 128], wkv[:, j, h, 128:256],
                            start=(j == 0), stop=(j == 1)), reads=["wkv", ("ckv", j, tk // 4)], writes=[("ps", b)])
                S.op("act", lambda e, b=b, t4=t4: e.activation(
                    Vb[:, t4 * 4:(t4 + 1) * 4, :], ps[b][:].rearrange("p (i d) -> p i d", i=4), AF.Copy),
                    reads=[("ps", b)], writes=[("V", t4)])
            for tc in range(TC):
                b = bank("hp", pb)
                for j in range(3):
                    S.op("pe", lambda e, j=j, b=b, tc=tc: e.matmul(ps[b][:], wq[:, j, h, 0:128], cqn[:, j, tcs(tc)],
                                                                 start=(j == 0), stop=(j == 2)),
                         reads=["wq", ("cq", j, tc)], writes=[("ps", b)])
                S.op("act", lambda e, b=b, tc=tc: e.activation(qn[:, tcs(tc)], ps[b][:], AF.Copy),
                     reads=[("ps", b)], writes=[("qn", tc)])
                b = bank("hp", pb)
                for j in range(3):
                    S.op("pe", lambda e, j=j, b=b, tc=tc: e.matmul(ps[b][:], wqr[:, j, h, :], cqn[:, j, tcs(tc)],
                                                                 start=(j == 0), stop=(j == 2)),
                         reads=["wqr", ("cq", j, tc)], writes=[("ps", b)])
                S.op("dve", lambda e, b=b, tc=tc: e.tensor_tensor(qr[:, tcs(tc)], ps[b][:], cs[:, tcs(tc)], ALU.mult),
                     reads=[("ps", b), ("cs", tc)], writes=[("qr", tc)])

        def attention(h):
            for Q in range(TC):
                att_q(h, Q)

        def att_q(h, Q):
            if True:
                q0 = Q * 512
                nk = 4 * Q + 4
                bo = bank("ao", [3, 4])
                bd = bank("ad", [5, 6])
                sinfo = {}

                def score(kc):
                    j = kc - 4 * Q
                    c0 = 128 * j if j > 0 else 0
                    bs = bank("as", [0, 1, 2])
                    k0 = kc * 128
                    S.op("pe", lambda e: e.matmul(ps[bs][:, c0:512], kTb[:, k0:k0 + 128], qn[:, q0 + c0:q0 + 512],
                                                  start=True, stop=False),
                         reads=[("kT", kc // 4), ("qn", Q)], writes=[("ps", bs)])
                    S.op("pe", lambda e: e.matmul(ps[bs][:, c0:512], kro[:, k0:k0 + 128], qr[:, q0 + c0:q0 + 512],
                                                  start=False, stop=True),
                         reads=[("kro", kc // 4), ("qr", Q)], writes=[("ps", bs)])
                    p = nxt("pt", NPT)
                    S.op("act", lambda e: e.activation(pt[p][:, c0:512], ps[bs][:, c0:512], AF.Exp, scale=SM_SCALE),
                         reads=[("ps", bs)], writes=[("pt", p)])
                    if j >= 0:
                        S.op("dve", lambda e: e.tensor_tensor(pt[p][:, c0:c0 + 128], pt[p][:, c0:c0 + 128], tri, ALU.mult),
                             reads=["cbf"], writes=[("pt", p)])
                    sinfo[kc] = (p, c0)

                def pv(kc):
                    p, c0 = sinfo[kc]
                    S.op("pe", lambda e: e.matmul(ps[bo][:, c0:512], Vb[:, kc, :], pt[p][:, c0:512],
                                                  start=(kc == 0), stop=(kc == nk - 1)),
                         reads=[("V", kc // 4), ("pt", p)], writes=[("ps", bo)])
                    S.op("pe", lambda e: e.matmul(ps[bd][:, c0:512], ones, pt[p][:, c0:512],
                                                  start=(kc == 0), stop=(kc == nk - 1)),
                         reads=["cbf", ("pt", p)], writes=[("ps", bd)])

                score(0)
                if nk > 1:
                    score(1)
                for kc in range(nk):
                    if kc + 2 < nk:
                        score(kc + 2)
                    pv(kc)
                r = nxt("ft", NFT)
                S.op("act", lambda e, r=r, bd=bd: e.activation(ft[r][:, 0:512], ps[bd][:], AF.Ln),
                     reads=[("ps", bd)], writes=[("ft", r)])
                S.op("act", lambda e, r=r: e.activation(ft[r][:, 0:512], ft[r][:, 0:512], AF.Exp, scale=-1.0), writes=[("ft", r)])
                S.op("dve", lambda e, r=r, bo=bo, Q=Q: e.tensor_tensor(hT[:, 4 + h, tcs(Q)], ps[bo][:], ft[r][:, 0:512], ALU.mult),
                     reads=[("ps", bo), ("ft", r)], writes=[("h", 4 + h, Q)])

        def out_proj(l, li, after=None):
            par = l % 2
            for tc in range(TC):
                for dc in range(8):
                    b = bank("op", [0, 1, 2, 3, 4, 5, 6])
                    for k in range(8):
                        S.op("pe", lambda e, k=k, dc=dc, b=b, tc=tc: e.matmul(
                            ps[b][:], w_out_v[:, k, dc * 128:(dc + 1) * 128], hT[:, k, tcs(tc)],
                            start=(k == 0), stop=(k == 7)), reads=["W1", ("h", k, tc)], writes=[("ps", b)])
                    S.op("dve", lambda e, dc=dc, b=b, tc=tc: e.scalar_tensor_tensor(
                        xT[:, dc, tcs(tc)], ps[b][:], coefG[par][:, 8 + dc:8 + dc + 1], xT[:, dc, tcs(tc)],
                        ALU.mult, ALU.add), reads=[("ps", b), ("cG", par, 1)], writes=[("x", dc, tc)])
                if after is not None and tc >= 1:
                    after(tc - 1)
            if after is not None:
                after(TC - 1)

        ffn_load(0)
        ffn_load(1)
        for pc in range(3):
            ada_dma(0, pc)
            ada_mm(0, pc)
        ada_finish(0, [0])
        gidx = 0
        stop = DBG["stop"]
        dbg_ops = []

        def dump(i, ap, reads):
            dbg_ops.append(S.op("pool", lambda e: e.dma_start(out=dbg_out[i, :, 0:ap.shape[1]], in_=ap), reads=reads, dma=True))

        if stop == 1:
            dump(0, cs[:], [("cs", tc) for tc in range(TC)])
            dump(1, mods[0][:], [("mods", 0, 0)])
            dump(2, coefA[0][:], [("cA", 0, s_) for s_ in range(3)])
            dump(3, coefG[0][:], [("cG", 0, s_) for s_ in range(3)])
        for li, l in enumerate(layers):
            if stop == 1:
                break
            if stop == 7:
                ffn_load(0)
                norm_mod(l, 0)
                for k in range(8):
                    S.op("pe", lambda e, k=k: e.matmul(ps[0][:], slot[0][:, k * 512:k * 512 + 128], hT[:, k, 0:512], start=(k == 0), stop=(k == 7)),
                         reads=[("slot", 0), ("slotu", 0), ("slotd", 0), ("h", k, 0)], writes=[("ps", 0)])
                S.op("dve", lambda e: e.tensor_copy(ft[0][:, 0:512], ps[0][:]), reads=[("ps", 0)], writes=[("ft", 0)])
                for i_, k_ in enumerate((2, 4, 6, 7)):
                    S.op("pe", lambda e, i_=i_, k_=k_: e.matmul(ps[1 + i_][:], slot[0][:, k_ * 512:k_ * 512 + 128], hT[:, k_, 0:512], start=True, stop=True),
                         reads=[("h", k_, 0), ("slot", 0)], writes=[("ps", 1 + i_)])
                    S.op("dve", lambda e, i_=i_: e.tensor_copy(ft[i_][:, 0:512], ps[1 + i_][:]), reads=[("ps", 1 + i_)], writes=[("ft", i_)])
                for i_ in range(4):
                    dump(i_, ft[i_][:, 0:512], [("ft", i_)])
                break
            if stop == 2:
                norm_mod(l, 0)
                for k_ in range(4):
                    dump(k_, hT[:, k_, :], [("h", k_, tc) for tc in range(TC)])
                break
            ffn_load(gidx)
            ffn_load(gidx + 1)
            if li > 0:
                mixer_load(li)
            else:
                norm_mod(l, 0)
            build_at = 3 if li > 0 else 9
            for gi in range(NGRP):
                ffn_group(l, 0, gidx, after=(lambda tc, l=l: norm_tc(l, 1, tc)) if gi == NGRP - 1 else None)
                if li == 0:
                    if 1 <= gi <= 6:
                        ada_mm(0, 2 + gi)
                    if gi <= 5:
                        ada_dma(0, 3 + gi)
                    if gi == 6:
                        ada_finish(0, [1, 2])
                        mixer_load(li)
                if gi == build_at:
                    mixer_build()
                if stop == 6:
                    dump(0, mid[:, 0, :], [("mid", 0, tc) for tc in range(TC)])
                    dump(1, mid[:, 1, :], [("mid", 1, tc) for tc in range(TC)])
                    dump(2, xT[:, 0, :], [("x", 0, tc) for tc in range(TC)])
                    dump(3, wgu_v[0][:, 1, :], [("slot", 0), ("slotu", 0)])
                    break
                if gi + 2 < NGRP:
                    ffn_load(gidx + 2)
                gidx += 1
            if stop == 6:
                break
            barrier()
            if stop == 3:
                break
            S.op("dve", lambda e: e.memset(hal[:], 0.0), writes=["hal_init"] + [("hal", g) for g in range(4)])
            pp = None
            for tc in range(TC):
                pp = phase_a(l, tc, pp)
            pool_y(l, *pp)
            S.op("pool", lambda e, li=li: e.dma_start(out=w_out_v, in_=w_out_d[li].rearrange("(k p) f -> p k f", p=128)),
                 writes=["W1"], dma=True)
            for h in range(NH):
                head_proj(h)
                attention(h)
            if stop == 4:
                for k_ in range(4):
                    dump(k_, hT[:, k_, :], [("h", k_, tc) for tc in range(TC)])
                break
            barrier()
            ffn_load(gidx)
            ffn_load(gidx + 1)
            out_proj(l, li, after=lambda tc, l=l: norm_tc(l, 2, tc))
            if stop == 5:
                break
            for gi in range(NGRP):
                aft = None
                if gi == NGRP - 1:
                    if li + 1 < nl:
                        aft = lambda tc, l=l: norm_tc(l + 1, 0, tc)
                    elif last:
                        aft = final_tc
                ffn_group(l, 2, gidx, after=aft)
                ffn_load(gidx + 2)
                gidx += 1
                if li + 1 < nl:
                    if 1 <= gi <= 9:
                        ada_mm(li + 1, gi - 1)
                    if gi <= 8:
                        ada_dma(li + 1, gi)
                    if gi == 9:
                        ada_finish(li + 1, [0, 1, 2])
        finals = []
        yv = yT_out.rearrange("(k p) t -> p k t", p=128)
        for k in range(8):
            finals.append(S.op("sp", lambda e, k=k: e.dma_start(out=yv[:, k, :], in_=xT[:, k, :]),
                               reads=[("x", k, tc) for tc in range(TC)], dma=True))
        S.emit(block, sems, finals + dbg_ops)
    return nc


def _host_consts():
    r = np.arange(128)
    inv_freq = 1.0 / (10000.0 ** (np.arange(0, 64, 2, dtype=np.float32) / 64.0))
    invf = inv_freq[(r % 64) % 32].astype(np.float32)
    phase = np.where(r < 64, math.pi / 2, 0.0).astype(np.float32)
    invcnt = np.zeros((128, 64), np.float32)
    for g in range(4):
        w = 2 << g
        for t in range(16):
            invcnt[:, g * 16 + t] = float(w) / min(t + 1, w)
    ones = np.ones((128, 128), np.float32)
    fold = np.zeros((128, 128), np.float32)
    for p in range(128):
        for m in range(128):
            if p % 64 == m % 64:
                fold[p, m] = 1.0
    tri = (r[None, :] >= r[:, None]).astype(np.float32)
    cbf = np.concatenate([ones, fold, tri], axis=1)
    return invf, phase, invcnt, np.ascontiguousarray(cbf)


LAUNCH_GROUPS = [[0, 1, 2, 3]]


def kernel(x, c, positions, ada_w, ada_b, ffn1_norm, ffn1_w_gate, ffn1_w_up, ffn1_w_down,
           mix_norm, w_in, pool_w, pool_scale, q_a_norm, w_q_b, kv_a_norm, w_kv_b, w_out,
           ffn2_norm, ffn2_w_gate, ffn2_w_up, ffn2_w_down, final_norm):
    f = lambda a: np.ascontiguousarray(np.asarray(a, dtype=np.float32))
    invf, phase, invcnt, cbf = _host_consts()
    smalls = np.zeros((128, NS), np.float32)

    def put(name, arr):
        o = SM_OFF[name]
        smalls[:, o:o + arr.shape[1]] = arr

    put("ada_b", np.concatenate([_pm(np.asarray(ada_b)[l]) for l in range(L)], axis=1))
    put("n1", np.concatenate([_pm(np.asarray(ffn1_norm)[l]) for l in range(L)], axis=1))
    put("n2", np.concatenate([_pm(np.asarray(mix_norm)[l]) for l in range(L)], axis=1))
    put("n3", np.concatenate([_pm(np.asarray(ffn2_norm)[l]) for l in range(L)], axis=1))
    put("pscale", np.concatenate([_pm(np.asarray(pool_scale)[l]) for l in range(L)], axis=1))
    put("qn", np.concatenate([_pm(np.asarray(q_a_norm)[l]) for l in range(L)], axis=1))
    put("kvn", np.concatenate([_pm(np.asarray(kv_a_norm)[l]) for l in range(L)], axis=1))
    put("fn", _pm(np.asarray(final_norm)))
    put("invf", invf[:, None])
    put("phase", phase[:, None])
    put("invcnt", invcnt)

    x = np.asarray(x, dtype=np.float32)
    c = np.asarray(c, dtype=np.float32)
    positions = np.asarray(positions, dtype=np.int32)
    cur = [np.ascontiguousarray(x[b].T) for b in range(NB)]
    wfull = dict(ada_w=f(ada_w), ffn1_w_gate=f(ffn1_w_gate), ffn1_w_up=f(ffn1_w_up), ffn1_w_down=f(ffn1_w_down),
                 w_in=f(w_in), pool_w=f(pool_w), w_q_b=f(w_q_b), w_kv_b=f(w_kv_b), w_out=f(w_out),
                 ffn2_w_gate=f(ffn2_w_gate), ffn2_w_up=f(ffn2_w_up), ffn2_w_down=f(ffn2_w_down))
    ng = len(LAUNCH_GROUPS)
    for gi, layers in enumerate(LAUNCH_GROUPS):
        nc = build(layers, gi == 0, gi == ng - 1)
        wl = {k: np.ascontiguousarray(v[layers[0]:layers[-1] + 1]) for k, v in wfull.items()}
        in_maps = []
        for b in range(NB):
            m = dict(wl)
            m.update(xT=cur[b], smalls=smalls, cvec=_pm(c[b]), cbf=cbf,
                     pos=np.ascontiguousarray(positions[b][None, :]))
            in_maps.append(m)
        res = run_bass_kernel_spmd(nc, in_maps, core_ids=list(range(NB)))
        cur = [np.asarray(res.results[b]["yT"]) for b in range(NB)]
    out = np.stack([cur[b].T for b in range(NB)], axis=0).astype(np.float32)
    return out
```
